# Optimizing a Trainium2 kernel written in Bass

```python
import jax, jax.numpy as jnp
from jax import lax
import numpy as np

D_MODEL = 2048
BATCH = 1
SEQ = 8192
DEPTH = 1

HG_HEADS = 8
HG_HEAD_DIM = 128
HG_WIDTH = HG_HEADS * HG_HEAD_DIM
HG_CHUNK = 64
ATT_HEADS = 16
ATT_KV_HEADS = 4
ATT_HEAD_DIM = 64
ATT_GROUP = ATT_HEADS // ATT_KV_HEADS
ATT_WIDTH = ATT_HEADS * ATT_HEAD_DIM
KV_WIDTH = ATT_KV_HEADS * ATT_HEAD_DIM
WINDOW = 128
ATT_BLOCK = 128
ROPE_THETA = 10000.0
D_FF = 4 * D_MODEL
ALPHA = (2 * DEPTH) ** 0.25
BETA = (8 * DEPTH) ** -0.25
LN_EPS = 1e-5
RMS_EPS = 1e-6

SPLIT_SIZES = (HG_WIDTH, HG_WIDTH, HG_WIDTH, HG_WIDTH,
               ATT_WIDTH, KV_WIDTH, KV_WIDTH,
               D_MODEL, D_MODEL)
D_IN = sum(SPLIT_SIZES)

kernel_name = "hgrn2_swa_sink_gated_hybrid"


def _split_points():
    pts, acc = [], 0
    for s in SPLIT_SIZES[:-1]:
        acc += s
        pts.append(acc)
    return pts


def layer_norm(x, gain, bias):
    xf = x.astype(jnp.float32)
    mu = jnp.mean(xf, axis=-1, keepdims=True)
    var = jnp.mean(jnp.square(xf - mu), axis=-1, keepdims=True)
    y = (xf - mu) * lax.rsqrt(var + LN_EPS) * gain.astype(jnp.float32) + bias.astype(jnp.float32)
    return y.astype(x.dtype)


def hgrn2_mixer(q, f_logit, i, g, lb, norm_gain):
    B, S, _ = q.shape
    f32 = jnp.float32
    lb = lb.astype(f32)
    f = lb + (1.0 - lb) * jax.nn.sigmoid(f_logit.astype(f32))
    log_f = jnp.log(f)
    k = 1.0 - f
    qf = jax.nn.silu(q.astype(f32))
    nc = S // HG_CHUNK

    def to_chunks(t):
        return t.reshape(B, nc, HG_CHUNK, HG_HEADS, HG_HEAD_DIM).transpose(1, 0, 3, 2, 4)

    qc, kc, vc, lc = to_chunks(qf), to_chunks(k), to_chunks(i.astype(f32)), to_chunks(log_f)
    causal = jnp.tril(jnp.ones((HG_CHUNK, HG_CHUNK), dtype=bool))[:, :, None]

    def step(state, inp):
        q_, k_, v_, l_ = inp
        b = jnp.cumsum(l_, axis=2)
        o_inter = jnp.einsum('bhtk,bhkv->bhtv', q_ * jnp.exp(b), state)
        diff = b[:, :, :, None, :] - b[:, :, None, :, :]
        decay = jnp.exp(jnp.where(causal, diff, -jnp.inf))
        scores = jnp.einsum('bhtk,bhtsk,bhsk->bhts', q_, decay, k_)
        o = o_inter + jnp.einsum('bhts,bhsv->bhtv', scores, v_)
        b_last = b[:, :, -1:, :]
        k_dec = k_ * jnp.exp(b_last - b)
        state = jnp.exp(b_last[:, :, 0, :])[..., None] * state + jnp.einsum('bhsk,bhsv->bhkv', k_dec, v_)
        return state, o

    state0 = jnp.zeros((B, HG_HEADS, HG_HEAD_DIM, HG_HEAD_DIM), f32)
    _, oc = lax.scan(step, state0, (qc, kc, vc, lc))
    o = oc.transpose(1, 0, 3, 2, 4).reshape(B, S, HG_HEADS, HG_HEAD_DIM)
    o = o * lax.rsqrt(jnp.mean(jnp.square(o), axis=-1, keepdims=True) + RMS_EPS)
    o = o.reshape(B, S, HG_WIDTH) * norm_gain.astype(f32)
    o = o * jax.nn.silu(g.astype(f32))
    return o.astype(q.dtype)


def rope(t, pos):
    half = t.shape[-1] // 2
    inv = ROPE_THETA ** (-jnp.arange(half, dtype=jnp.float32) / half)
    ang = pos.astype(jnp.float32)[:, None] * inv[None, :]
    cos = jnp.cos(ang)[None, :, None, :]
    sin = jnp.sin(ang)[None, :, None, :]
    tf = t.astype(jnp.float32)
    t1, t2 = tf[..., :half], tf[..., half:]
    return jnp.concatenate([t1 * cos - t2 * sin, t2 * cos + t1 * sin], axis=-1).astype(t.dtype)


def swa_sink_attention(q, k, v, sinks):
    B, S = q.shape[:2]
    f32 = jnp.float32
    nb = S // ATT_BLOCK
    qb = q.reshape(B, nb, ATT_BLOCK, ATT_KV_HEADS, ATT_GROUP, ATT_HEAD_DIM).astype(f32)

    def with_prev(t):
        tb = t.reshape(B, nb, ATT_BLOCK, ATT_KV_HEADS, ATT_HEAD_DIM).astype(f32)
        prev = jnp.pad(tb, ((0, 0), (1, 0), (0, 0), (0, 0), (0, 0)))[:, :-1]
        return jnp.concatenate([prev, tb], axis=2)

    kb, vb = with_prev(k), with_prev(v)
    scale = ATT_HEAD_DIM ** -0.5
    scores = jnp.einsum('bnqkgd,bnskd->bnkgqs', qb, kb) * scale
    qi = jnp.arange(ATT_BLOCK)[:, None] + ATT_BLOCK
    si = jnp.arange(2 * ATT_BLOCK)[None, :]
    rel = qi - si
    band = (rel >= 0) & (rel < WINDOW)
    key_pos = jnp.arange(nb)[:, None] * ATT_BLOCK - ATT_BLOCK + si
    valid = band[None] & (key_pos >= 0)[:, None, :]
    scores = jnp.where(valid[None, :, None, None], scores, -jnp.inf)
    sink = sinks.astype(f32).reshape(ATT_KV_HEADS, ATT_GROUP)[None, None, :, :, None, None]
    m = jnp.maximum(jnp.max(scores, axis=-1, keepdims=True), sink)
    p = jnp.exp(scores - m)
    probs = p / (jnp.sum(p, axis=-1, keepdims=True) + jnp.exp(sink - m))
    out = jnp.einsum('bnkgqs,bnskd->bnqkgd', probs, vb)
    return out.reshape(B, S, ATT_WIDTH).astype(q.dtype)


def setup_inputs(seed: int = 0) -> dict:
    key = jax.random.key(seed)
    ks = jax.random.split(key, 16)
    f32 = jnp.float32
    x = jax.random.normal(ks[0], (BATCH, SEQ, D_MODEL), f32)
    w_in = jax.random.normal(ks[1], (DEPTH, D_MODEL, D_IN), f32) * D_MODEL ** -0.5
    hg_lb_logits = jax.random.normal(ks[2], (DEPTH + 1, HG_WIDTH), f32) * 0.5
    hg_norm_gain = 1.0 + 0.02 * jax.random.normal(ks[3], (DEPTH, HG_WIDTH), f32)
    attn_sinks = jax.random.normal(ks[4], (DEPTH, ATT_HEADS), f32)
    w_branch_a = jax.random.normal(ks[5], (DEPTH, HG_WIDTH, D_MODEL), f32) * (HG_WIDTH ** -0.5) * BETA
    w_branch_b = jax.random.normal(ks[6], (DEPTH, ATT_WIDTH, D_MODEL), f32) * (ATT_WIDTH ** -0.5) * BETA
    w_out = jax.random.normal(ks[7], (DEPTH, D_MODEL, D_MODEL), f32) * (D_MODEL ** -0.5) * BETA
    ln1_gain = 1.0 + 0.02 * jax.random.normal(ks[8], (DEPTH, D_MODEL), f32)
    ln1_bias = 0.02 * jax.random.normal(ks[9], (DEPTH, D_MODEL), f32)
    w_ff1 = jax.random.normal(ks[10], (DEPTH, D_MODEL, D_FF), f32) * (D_MODEL ** -0.5) * BETA
    w_ff2 = jax.random.normal(ks[11], (DEPTH, D_FF, D_MODEL), f32) * (D_FF ** -0.5) * BETA
    ln2_gain = 1.0 + 0.02 * jax.random.normal(ks[12], (DEPTH, D_MODEL), f32)
    ln2_bias = 0.02 * jax.random.normal(ks[13], (DEPTH, D_MODEL), f32)
    return {"x": x, "w_in": w_in, "hg_lb_logits": hg_lb_logits, "hg_norm_gain": hg_norm_gain,
            "attn_sinks": attn_sinks, "w_branch_a": w_branch_a, "w_branch_b": w_branch_b,
            "w_out": w_out, "ln1_gain": ln1_gain, "ln1_bias": ln1_bias, "w_ff1": w_ff1,
            "w_ff2": w_ff2, "ln2_gain": ln2_gain, "ln2_bias": ln2_bias}


def reference(x, w_in, hg_lb_logits, hg_norm_gain, attn_sinks, w_branch_a, w_branch_b,
              w_out, ln1_gain, ln1_bias, w_ff1, w_ff2, ln2_gain, ln2_bias):
    B, S, _ = x.shape
    pos = jnp.arange(S)
    lb_all = jnp.cumsum(jax.nn.softmax(hg_lb_logits.astype(jnp.float32), axis=0), axis=0)
    splits = _split_points()
    for l in range(DEPTH):
        proj = x @ w_in[l]
        hq, hf, hi, hg, aq, ak, av, ga, gb = jnp.split(proj, splits, axis=-1)
        y_a = hgrn2_mixer(hq, hf, hi, hg, lb_all[l], hg_norm_gain[l]) @ w_branch_a[l]
        q = rope(aq.reshape(B, S, ATT_HEADS, ATT_HEAD_DIM), pos)
        k = rope(ak.reshape(B, S, ATT_KV_HEADS, ATT_HEAD_DIM), pos)
        v = av.reshape(B, S, ATT_KV_HEADS, ATT_HEAD_DIM)
        y_b = swa_sink_attention(q, k, v, attn_sinks[l]) @ w_branch_b[l]
        mixed = jax.nn.sigmoid(ga) * y_a + jax.nn.sigmoid(gb) * y_b
        x = layer_norm(ALPHA * x + mixed @ w_out[l], ln1_gain[l], ln1_bias[l])
        h = jnp.square(jax.nn.relu(x @ w_ff1[l])) @ w_ff2[l]
        x = layer_norm(ALPHA * x + h, ln2_gain[l], ln2_bias[l])
    return x
```

```python
import numpy as np
import concourse.bass as bass
import concourse.mybir as mybir
from concourse.bass_utils import run_bass_kernel_spmd
from contextlib import ExitStack

F32 = mybir.dt.float32
BF16 = mybir.dt.bfloat16
AF = mybir.ActivationFunctionType
ALU = mybir.AluOpType
AX = mybir.AxisListType

NCORES = 8
S_TOT = 8192
TPC = 1024
D = 2048
DFF = 8192
HGW = 1024
ALPHA = 2.0 ** 0.25
LN_EPS = 1e-5
RMS_EPS = 1e-6
NEG = -30000.0
XW = TPC + 128

C_HQ, C_HF, C_HI, C_HG, C_AQ, C_AK, C_AV, C_GA, C_GB = 0, 1024, 2048, 3072, 4096, 5120, 5376, 5632, 7680

K_ID, K_CM, K_RS, K_M0, K_M1, K_COS, K_SIN = 0, 128, 256, 768, 1024, 1280, 2432
K_RM, K_L0, K_L1, K_GN, K_SK = 3584, 3592, 3600, 3608, 3616
NCST = 3632


class Buf:
    __slots__ = ("name", "last_w", "readers", "excl")

    def __init__(self, name, excl=False):
        self.name = name
        self.last_w = None
        self.readers = []
        self.excl = excl


class Op:
    __slots__ = ("eng", "fn", "deps", "is_dma", "sig", "sem", "cnt", "idx", "waits", "inc", "sw", "retire_of")

    def __init__(self, eng, fn, is_dma):
        self.eng = eng
        self.fn = fn
        self.deps = []
        self.is_dma = is_dma
        self.sig = False
        self.sem = None
        self.cnt = 0
        self.waits = []
        self.inc = 1
        self.sw = False
        self.retire_of = None


ENGS = ("pe", "act", "dve", "pool", "sp")


class Prog:
    def __init__(self, nc):
        self.nc = nc
        self.ops = []
        self.bar = None
        self.last = {}
        self.dmas = []
        self.sw_pending = None

    def op(self, eng, fn, reads=(), writes=(), is_dma=False):
        o = Op(eng, fn, is_dma)
        o.idx = len(self.ops)
        deps = set()
        ex = [b for b in reads if b.excl]
        if ex:
            reads = [b for b in reads if not b.excl]
            writes = list(writes) + [b for b in ex if b not in writes]
        for b in reads:
            if b.last_w is not None:
                deps.add(b.last_w)
        for b in writes:
            if b.last_w is not None:
                deps.add(b.last_w)
            for r in b.readers:
                deps.add(r)
        if self.bar is not None:
            deps.add(self.bar)
        o.deps = sorted(deps, key=lambda d: d.idx)
        for b in reads:
            b.readers.append(o)
        for b in writes:
            b.last_w = o
            b.readers = []
        self.ops.append(o)
        self.last[eng] = o
        if is_dma:
            self.dmas.append(o)
        return o

    def dma(self, q, out, in_, reads=(), writes=(), **kw):
        if q == "pool":
            return self.swdma(out, in_, reads, writes, **kw)
        return self.op(q, lambda e: e.dma_start(out=out, in_=in_, **kw), reads, writes, is_dma=True)

    def swdma(self, out, in_, reads=(), writes=(), **kw):
        o = self.op("pool", lambda e: e.dma_start(out=out, in_=in_, **kw), reads, writes, is_dma=True)
        o.sw = True
        return o

    def _retire(self, pend):
        d, writes = pend
        r = self.op("pool", None, [], writes)
        r.retire_of = d
        return r

    def sw_flush(self):
        if self.sw_pending is not None:
            self._retire(self.sw_pending)
            self.sw_pending = None

    def barrier(self):
        self.sw_flush()
        hub = Op("sp", lambda e: e.nop(), False)
        hub.idx = len(self.ops)
        deps = set(self.last.values()) | set(self.dmas)
        if self.bar is not None:
            deps.add(self.bar)
        hub.deps = sorted(deps, key=lambda d: d.idx)
        self.ops.append(hub)
        self.last = {"sp": hub}
        self.dmas = []
        self.bar = hub
        return hub

    def emit(self, final_wait_ops=()):
        nc = self.nc
        self.sw_flush()
        for o in self.ops:
            for d in o.deps:
                if d.eng == "pe" and o.eng == "pe" and not d.is_dma:
                    continue
                d.sig = True
        for o in final_wait_ops:
            o.sig = True
        eng_cnt = {e: 0 for e in ENGS}
        eng_sem = {e: nc.alloc_semaphore(name="es_" + e) for e in ENGS}
        NDMA = 4
        dma_pool = [nc.alloc_semaphore(name="ds%d" % i) for i in range(NDMA)]
        dma_cnt = [0] * NDMA
        dma_last = [None] * NDMA
        rr = 0
        swn = 0
        for o in self.ops:
            if o.sw:
                o.sig = True
                o.sem, o.cnt, o.inc = nc.alloc_semaphore(name="sw%d" % swn), 16, 16
                swn += 1
                continue
            if not o.sig:
                continue
            if o.is_dma:
                k = rr % NDMA
                rr += 1
                if dma_last[k] is not None:
                    o.deps.append(dma_last[k])
                dma_last[k] = o
                dma_cnt[k] += 16
                o.sem, o.cnt, o.inc = dma_pool[k], dma_cnt[k], 16
            else:
                eng_cnt[o.eng] += 1
                o.sem, o.cnt, o.inc = eng_sem[o.eng], eng_cnt[o.eng], 1
        waited = {e: {} for e in ENGS}
        for o in self.ops:
            w = waited[o.eng]
            need = {}
            for d in o.deps:
                if d.eng == "pe" and o.eng == "pe" and not d.is_dma:
                    continue
                key = id(d.sem)
                if w.get(key, 0) >= d.cnt:
                    continue
                if key not in need or need[key][1] < d.cnt:
                    need[key] = (d.sem, d.cnt)
            for key, (s, c) in need.items():
                w[key] = c
                o.waits.append((s, c))
        per_eng = {e: [o for o in self.ops if o.eng == e] for e in ENGS}
        finals = {}
        for o in final_wait_ops:
            prev = finals.get(id(o.sem), (None, 0))[1]
            finals[id(o.sem)] = (o.sem, max(o.cnt, prev))

        def run(name, eng):
            for o in per_eng[name]:
                for (s, c) in o.waits:
                    eng.wait_ge(s, c)
                if o.retire_of is not None:
                    d = o.retire_of
                    if not any(id(s) == id(d.sem) for (s, c) in o.waits):
                        eng.wait_ge(d.sem, 16)
                    eng.sem_clear(d.sem)
                    inst = eng.nop()
                else:
                    inst = o.fn(eng)
                if o.sig:
                    inst.then_inc(o.sem, o.inc)

        with nc.Block() as block:
            @block.tensor
            def _(e):
                run("pe", e)

            @block.scalar
            def _(e):
                run("act", e)

            @block.vector
            def _(e):
                run("dve", e)

            @block.gpsimd
            def _(e):
                run("pool", e)

            @block.sync
            def _(e):
                run("sp", e)
                for (s, c) in finals.values():
                    e.wait_ge(s, c)
        st = {e: len(per_eng[e]) for e in ENGS}
        st["waits"] = sum(len(o.waits) for o in self.ops)
        st["sem_max"] = dict(eng_cnt)
        return st


def _weight_blocks():
    blks = []
    for h in range(8):
        blks.append(("P1_%d" % h, 4096))
    for g in range(4):
        blks.append(("AQ_%d" % g, 8192))
        blks.append(("AKV_%d" % g, 5120))
    for h in range(8):
        blks.append(("P2_%d" % h, 8192))
    for i in range(8):
        blks.append(("G_%d" % i, 8192))
        blks.append(("AB_%d" % i, 4096))
    for cb in range(4):
        blks.append(("WO_%d" % cb, 8192))
    for fb in range(16):
        blks.append(("F1_%d" % fb, 8192))
        blks.append(("F2_%d" % fb, 8192))
    offs = {}
    o = 0
    order = []
    for name, n in blks:
        offs[name] = (o, n)
        order.append(name)
        o += n
    return offs, order, o


W_OFFS, W_ORDER, W_TOT = _weight_blocks()
W_SPLIT = W_OFFS["F1_0"][0]


def _tile_w(W, cols):
    K = W.shape[0]
    blk = W[:, cols]
    n = blk.shape[1]
    return np.ascontiguousarray(blk.reshape(K // 128, 128, n).transpose(1, 0, 2)).reshape(128, (K // 128) * n)


def _build_wflat(w_in, w_a, w_b, w_out, w1, w2):
    wf = np.empty((128, W_TOT), np.float32)
    ar = np.arange

    def put(name, arr):
        o, n = W_OFFS[name]
        assert arr.shape == (128, n), (name, arr.shape, n)
        wf[:, o:o + n] = arr

    def perm64(base):
        return np.concatenate([base + 32 + ar(32), base + ar(32)])

    for h in range(8):
        put("P1_%d" % h, _tile_w(w_in, np.concatenate([C_HF + h * 128 + ar(128), C_HI + h * 128 + ar(128)])))
        put("P2_%d" % h, _tile_w(w_in, np.concatenate([C_HQ + h * 128 + ar(128), C_HF + h * 128 + ar(128),
                                                        C_HG + h * 128 + ar(128), C_HI + h * 128 + ar(128)])))
    for g in range(4):
        qc = C_AQ + g * 256 + ar(256)
        qp = np.concatenate([perm64(C_AQ + g * 256 + hh * 64) for hh in range(4)])
        put("AQ_%d" % g, _tile_w(w_in, np.concatenate([qc, qp])))
        kc = C_AK + g * 64 + ar(64)
        kp = perm64(C_AK + g * 64)
        vc = C_AV + g * 64 + ar(64)
        put("AKV_%d" % g, _tile_w(w_in, np.concatenate([kc, kc, kp, kp, vc])))
    for i in range(8):
        put("G_%d" % i, _tile_w(w_in, np.concatenate([C_GA + i * 256 + ar(256), C_GB + i * 256 + ar(256)])))
        ab = np.concatenate([_tile_w(w_a, i * 256 + ar(256)), _tile_w(w_b, i * 256 + ar(256))], axis=1)
        put("AB_%d" % i, ab)
    for cb in range(4):
        put("WO_%d" % cb, _tile_w(w_out, cb * 512 + ar(512)))
    for fb in range(16):
        put("F1_%d" % fb, _tile_w(w1, fb * 512 + ar(512)))
        put("F2_%d" % fb, _tile_w(w2[fb * 512:(fb + 1) * 512, :], ar(2048)))
    return wf


def _build_consts(core, lb_logits, gain, sinks):
    c = np.zeros((128, NCST), np.float32)
    p = np.arange(128)
    c[:, K_ID:K_ID + 128] = np.eye(128, dtype=np.float32)
    c[:, K_CM:K_CM + 128] = (p[:, None] <= p[None, :]).astype(np.float32)
    rs = np.ones(512, np.float32)
    rs[::128] = 0.0
    c[:, K_RS:K_RS + 512] = rs[None, :]
    q = p[:, None]
    s = np.arange(256)[None, :]
    valid = np.where(s < 128, s > q, (s - 128) <= q)
    m1 = np.where(valid, 0.0, NEG).astype(np.float32)
    m0 = m1.copy()
    if core == 0:
        m0[:, :128] = NEG
    c[:, K_M0:K_M0 + 256] = m0
    c[:, K_M1:K_M1 + 256] = m1
    d = p % 64
    j = d % 32
    inv = (np.float32(10000.0) ** (-np.arange(32, dtype=np.float32) / np.float32(32))).astype(np.float32)
    pos = (core * TPC - 128 + np.arange(XW)).astype(np.float32)
    ang = (pos[None, :] * inv[j][:, None]).astype(np.float32)
    c[:, K_COS:K_COS + XW] = np.cos(ang).astype(np.float32)
    sn = np.sin(ang).astype(np.float32)
    c[:, K_SIN:K_SIN + XW] = np.where((d < 32)[:, None], -sn, sn)
    c[:, K_RM:K_RM + 8] = (np.arange(8) < core).astype(np.float32)[None, :]
    c[:, K_L0:K_L0 + 8] = lb_logits[0].reshape(8, 128).T
    c[:, K_L1:K_L1 + 8] = lb_logits[1].reshape(8, 128).T
    c[:, K_GN:K_GN + 8] = gain.reshape(8, 128).T
    c[:, K_SK:K_SK + 16] = sinks.reshape(1, 16)
    return c


ARENA_W = 51200
NWB = 3


STOP_BLOCKS = {"P1": 8, "ATT": 16, "A": 24, "B": 40, "C": 44, None: len(W_ORDER)}


def wtot_for(stop_after):
    nb = STOP_BLOCKS[stop_after]
    o, n = W_OFFS[W_ORDER[nb - 1]]
    return o + n


def build_program(debug=False, stop_after=None):
    nc = bass.Bass("TRN2", target_bir_lowering=False)
    NBLK = STOP_BLOCKS[stop_after]

    def din(name, shape, dt=F32):
        return nc.dram_tensor(name, shape, dt, kind="ExternalInput").ap()

    xT_d = din("xT", [128, 16 * XW])
    xtok_d = din("xtok", [128, 8 * D])
    wflat_d = din("wflat", [128, min(wtot_for(stop_after), W_SPLIT)])
    wflat2_d = din("wflat2", [128, W_TOT - W_SPLIT]) if wtot_for(stop_after) > W_SPLIT else None
    cst_d = din("cst", [128, NCST])
    lnp_d = din("lnp", [4, D])
    out_d = nc.dram_tensor("out", [TPC, D], F32, kind="ExternalOutput").ap()
    xpre_d = din("xpre", [128, 14 * 16 * 512])
    dbg = {}
    if debug:
        dbg["oh"] = nc.dram_tensor("dbg_oh", [128, 8 * TPC], BF16, kind="ExternalOutput").ap()
        dbg["oatt"] = nc.dram_tensor("dbg_oatt", [128, 8 * TPC], BF16, kind="ExternalOutput").ap()
        dbg["mixed"] = nc.dram_tensor("dbg_mixed", [128, 16 * TPC], BF16, kind="ExternalOutput").ap()
        dbg["x1"] = nc.dram_tensor("dbg_x1", [128, 8 * D], F32, kind="ExternalOutput").ap()
        dbg["T"] = nc.dram_tensor("dbg_T", [128, 8 * 128], F32, kind="ExternalOutput").ap()

    P = Prog(nc)
    finals = []
    es = ExitStack()
    arena = es.enter_context(nc.sbuf_tensor("arena", [128, ARENA_W], F32))
    psum = es.enter_context(nc.psum_tensor("psum", [128, 8 * 512], F32))
    PB = [Buf("psb%d" % i, excl=True) for i in range(8)]

    def bank(i, n=1):
        return psum[:, i * 512:(i + n) * 512]

    def bank_bf(i):
        return psum[:, i * 512:(i + 1) * 512].bitcast(BF16)

    class Region:
        def __init__(self, lo, hi):
            self.lo, self.hi, self.top = lo, hi, lo

        def reset(self):
            self.top = self.lo

        def alloc(self, shape, dt=F32, name="t"):
            n = 1
            for s_ in shape:
                n *= s_
            words = n if dt == F32 else (n + 1) // 2
            words = (words + 7) // 8 * 8
            off = self.top
            self.top += words
            assert self.top <= self.hi, ("arena region overflow", name, self.top, self.hi)
            ap = arena[:, off:off + words]
            if dt != F32:
                ap = ap.bitcast(dt)
            ap = ap[:, 0:n]
            if len(shape) == 2:
                ap = ap.rearrange("p (a b) -> p a b", a=shape[0])
            elif len(shape) == 3:
                ap = ap.rearrange("p (a b c) -> p a b c", a=shape[0], b=shape[1])
            return ap, Buf(name)

    R_C = Region(0, 4096)
    R_W = Region(4096, 4096 + NWB * 4096)
    o_ = 4096 + NWB * 4096
    R_BIG = Region(o_, o_ + 16384)
    o_ += 16384
    R_ACT = Region(o_, o_ + 8192)
    o_ += 8192
    R_X = Region(o_, ARENA_W)
    assert ARENA_W - o_ >= 8192, (ARENA_W, o_)

    def OP(eng, method, reads, writes, *args, **kw):
        return P.op(eng, lambda e: getattr(e, method)(*args, **kw), reads, writes)

    def MM(out, lhsT, rhs, start, stop, reads, writes, **kw):
        return P.op("pe", lambda e: e.matmul(out, lhsT=lhsT, rhs=rhs, start=start, stop=stop, **kw), reads, writes)

    def TR(out, in_, ident, reads, writes):
        return P.op("pe", lambda e: e.transpose(out, in_, ident), reads, writes)

    def ACT(out, in_, func, reads, writes, **kw):
        return P.op("act", lambda e: e.activation(out=out, in_=in_, func=func, **kw), reads, writes)

    cst, B_cst = R_C.alloc([NCST], F32, "cst")
    P.dma("sp", cst, cst_d, writes=[B_cst])
    ident, B_id = R_C.alloc([128], BF16, "ident")
    ones_bf, B_ones = R_C.alloc([128], BF16, "ones")
    zeros_bf, B_zeros = R_C.alloc([128], BF16, "zeros")
    vec, B_vec = R_C.alloc([32], F32, "vec")
    OP("act", "copy", [B_cst], [B_id], out=ident, in_=cst[:, K_ID:K_ID + 128])
    OP("dve", "memset", [], [B_ones], ones_bf, 1.0)
    OP("dve", "memset", [], [B_zeros], zeros_bf, 0.0)
    lb = vec[:, 0:8]
    oml = vec[:, 8:16]
    OP("dve", "tensor_tensor", [B_cst], [B_vec], out=lb, in0=cst[:, K_L0:K_L0 + 8], in1=cst[:, K_L1:K_L1 + 8], op=ALU.subtract)
    ACT(lb, lb, AF.Sigmoid, [B_vec], [B_vec])
    OP("dve", "tensor_scalar", [B_vec], [B_vec], out=oml, in0=lb, scalar1=-1.0, scalar2=1.0, op0=ALU.mult, op1=ALU.add)
    OP("dve", "memset", [], [B_vec], vec[:, 16:17], RMS_EPS)
    OP("dve", "memset", [], [B_vec], vec[:, 17:18], LN_EPS)
    eps_rms = vec[:, 16:17]
    eps_ln = vec[:, 17:18]
    cmask = cst[:, K_CM:K_CM + 128]
    restart = cst[:, K_RS:K_RS + 512]
    gain = cst[:, K_GN:K_GN + 8]
    sinks = cst[:, K_SK:K_SK + 16]
    rmask = cst[:, K_RM:K_RM + 8]

    wslots = [R_W.alloc([8192], BF16, "wslot%d" % i) for i in range(NWB)]
    wstate = {"next": 0}

    def wload_next():
        i = wstate["next"]
        if i >= NBLK:
            return
        wstate["next"] = i + 1
        name = W_ORDER[i]
        off, n = W_OFFS[name]
        ap, b = wslots[i % NWB]
        src = wflat_d[:, off:off + n] if off < W_SPLIT else wflat2_d[:, off - W_SPLIT:off - W_SPLIT + n]
        P.dma("pool", ap[:, 0:n], src, writes=[b], max_dma_last_dim=4096)

    wuse = {"i": 0}

    def wget(name, K):
        i = wuse["i"]
        assert W_ORDER[i] == name, (W_ORDER[i], name)
        wuse["i"] = i + 1
        off, n = W_OFFS[name]
        ap, b = wslots[i % NWB]
        return ap[:, 0:n].rearrange("p (k c) -> p k c", k=K), b

    xT, B_xT = R_BIG.alloc([16, XW], BF16, "xT")
    o_hT, B_oh = R_BIG.alloc([8, TPC], BF16, "o_hT")
    T, B_T = R_BIG.alloc([8, 128], F32, "T")

    def proj_fm(ps_ap, ps_bufs, wv, wb, col0, ncol_part, xcol0, ntok, xsrc=None, Bx=None):
        if xsrc is None:
            xsrc, Bx = xT, B_xT
        for kc in range(16):
            MM(ps_ap, wv[:, kc, col0:col0 + ncol_part], xsrc[:, kc, xcol0:xcol0 + ntok], kc == 0, kc == 15,
               [wb, Bx], ps_bufs)

    def hgrn_tiles(R):
        t = {}
        for nm, shp, dt in [("f", [512], F32), ("lf", [512], F32), ("b", [512], F32), ("ed", [512], BF16),
                            ("ebl", [8], F32), ("kdec", [512], BF16), ("kdT", [4, 128], BF16),
                            ("vfm", [512], BF16), ("vtok", [4, 128], BF16)]:
            t[nm] = R.alloc(shp, dt, nm)
        return t

    def hgrn_proj(t, h, half, wv, wb, col_f, col_i, pb_f, pb_i, xsrc=None, Bx=None, x0=None):
        if x0 is None:
            x0 = 128 + half * 512
        proj_fm(bank(pb_f), [PB[pb_f]], wv, wb, col_f, 128, x0, 512, xsrc, Bx)
        proj_fm(bank(pb_i), [PB[pb_i]], wv, wb, col_i, 128, x0, 512, xsrc, Bx)

    def hgrn_s1(t, h, half, wv, wb, col_f, col_i, pb_f, pb_i, xsrc=None, Bx=None, x0=None):
        hgrn_proj(t, h, half, wv, wb, col_f, col_i, pb_f, pb_i, xsrc, Bx, x0)
        hgrn_chain(t, h, half, pb_f, pb_i)

    def hgrn_chain(t, h, half, pb_f, pb_i):
        f, Bf = t["f"]
        lf, Blf = t["lf"]
        b, Bb = t["b"]
        ed, Bed = t["ed"]
        ebl, Bebl = t["ebl"]
        kdec, Bkd = t["kdec"]
        vfm, Bvfm = t["vfm"]
        ACT(f, bank(pb_f), AF.Sigmoid, [PB[pb_f]], [Bf])
        OP("dve", "tensor_scalar", [Bf, B_vec], [Bf], out=f, in0=f, scalar1=oml[:, h:h + 1], scalar2=lb[:, h:h + 1],
           op0=ALU.mult, op1=ALU.add)
        ACT(lf, f, AF.Ln, [Bf], [Blf])
        OP("act", "copy", [PB[pb_i]], [Bvfm], out=vfm, in_=bank(pb_i))
        OP("dve", "tensor_scalar", [Bf], [Bf], out=f, in0=f, scalar1=-1.0, scalar2=1.0, op0=ALU.mult, op1=ALU.add)
        OP("dve", "tensor_tensor_scan", [Blf, B_cst], [Bb], out=b, data0=restart, data1=lf, initial=0.0,
           op0=ALU.mult, op1=ALU.add)
        ACT(ebl[:, half * 4:half * 4 + 4], b[:, 127::128], AF.Exp, [Bb], [Bebl])
        for c in range(4):
            ACT(ed[:, c * 128:(c + 1) * 128], b[:, c * 128:(c + 1) * 128], AF.Exp, [Bb], [Bed], scale=-1.0,
                bias=b[:, c * 128 + 127:c * 128 + 128])
        OP("dve", "tensor_tensor", [Bf, Bed], [Bkd], out=kdec, in0=f, in1=ed, op=ALU.mult)

    def hgrn_s2(t, pb_t, pb_s):
        hgrn_s2b(t, pb_t)
        hgrn_s2c(t, pb_s)

    def hgrn_s2c(t, pb_s):
        kdT, BkdT = t["kdT"]
        vtok, Bvt = t["vtok"]
        for c in range(4):
            MM(bank(pb_s)[:, c * 128:(c + 1) * 128], kdT[:, c, :], vtok[:, c, :], True, True, [BkdT, Bvt], [PB[pb_s]])

    def hgrn_s2b(t, pb_t):
        kdec, Bkd = t["kdec"]
        kdT, BkdT = t["kdT"]
        vfm, Bvfm = t["vfm"]
        vtok, Bvt = t["vtok"]
        tb = bank_bf(pb_t)
        for c in range(4):
            TR(tb[:, c * 128:(c + 1) * 128], kdec[:, c * 128:(c + 1) * 128], ident, [Bkd, B_id], [PB[pb_t]])
        for c in range(4):
            TR(tb[:, 512 + c * 128:512 + (c + 1) * 128], vfm[:, c * 128:(c + 1) * 128], ident, [Bvfm, B_id], [PB[pb_t]])
        OP("act", "copy", [PB[pb_t]], [BkdT, Bvt], out=kdT.rearrange("p a b -> p (a b)"), in_=tb[:, 0:512])
        OP("dve", "tensor_copy", [PB[pb_t]], [Bvt], out=vtok.rearrange("p a b -> p (a b)"), in_=tb[:, 512:1024])

    def hgrn_fchain(t, h, half, wv, wb, col_f, col_i, pb_f, pb_i, pb_t, pb_s, xsrc=None, Bx=None, x0=None):
        hgrn_s1(t, h, half, wv, wb, col_f, col_i, pb_f, pb_i, xsrc, Bx, x0)
        hgrn_s2(t, pb_t, pb_s)

    R_ACT.reset()
    R_X.reset()
    OP("dve", "memset", [], [B_T], T, 0.0)
    xh = [R_ACT.alloc([16, 512], BF16, "xh%d" % i) for i in range(2)]
    p1x, B_p1x = R_X.alloc([8192], BF16, "p1x")
    tls = [hgrn_tiles(R_X), hgrn_tiles(R_X)]
    p1w = []
    for h2 in range(4):
        off, n = W_OFFS["P1_%d" % (2 * h2)]
        ap, b = wslots[h2] if h2 < 3 else (p1x, B_p1x)
        P.dma("pool", ap, wflat_d[:, off:off + 8192], writes=[b], max_dma_last_dim=4096)
        for j in range(2):
            p1w.append((ap[:, j * 4096:(j + 1) * 4096].rearrange("p (k c) -> p k c", k=16), b))
    wstate["next"] = 8
    wuse["i"] = 8

    def prefix_c(u):
        sh, h = divmod(u, 8)
        par = u % 2
        t = tls[par]
        pb_s = 3 + par * 4
        hgrn_s2c(t, pb_s)
        for c in range(4):
            OP("dve", "scalar_tensor_tensor", [B_T, t["ebl"][1], PB[pb_s]], [B_T], out=T[:, h, :],
               in0=T[:, h, :], scalar=t["ebl"][0][:, c:c + 1],
               in1=bank(pb_s)[:, c * 128:(c + 1) * 128], op0=ALU.mult, op1=ALU.add)

    NU = 14 * 8
    for u in range(NU + 2):
        if u < NU:
            sh, h = divmod(u, 8)
            xb_, Bxb_ = xh[sh % 2]
            if h == 0:
                P.dma("pool", xb_, xpre_d[:, sh * 8192:(sh + 1) * 8192].rearrange("p (k t) -> p k t", k=16), writes=[Bxb_],
                      max_dma_last_dim=4096)
                if sh == 1:
                    P.dma("pool", xT, xT_d.rearrange("p (k t) -> p k t", k=16), writes=[B_xT], max_dma_last_dim=4096)
            wv, wb = p1w[h]
            par = u % 2
            hgrn_proj(tls[par], h, 0, wv, wb, 0, 128, 0 + par * 4, 1 + par * 4, xb_, Bxb_, 0)
        if 0 <= u - 2 < NU:
            prefix_c(u - 2)
        if 0 <= u - 1 < NU:
            hgrn_s2b(tls[(u - 1) % 2], 2 + ((u - 1) % 2) * 4)
        if u < NU:
            hgrn_chain(tls[par], h, 0, 0 + par * 4, 1 + par * 4)
    if debug:
        finals.append(P.dma("sp", dbg["T"], T.rearrange("p a b -> p (a b)"), reads=[B_T], writes=[Buf("d")]))
    P.barrier()
    for _ in range(NWB):
        wload_next()
    R_ACT.reset()
    R_X.reset()
    o_attT, B_oa = R_X.alloc([8, TPC], BF16, "o_attT")
    x_mark = R_X.top
    if stop_after == "P1":
        st = P.emit(finals)
        es.close()
        return nc, st

    R_ACT.reset()
    cosT = cst[:, K_COS:K_COS + XW]
    sinT = cst[:, K_SIN:K_SIN + XW]
    kT, B_kT = R_ACT.alloc([XW], BF16, "kT")
    vat, B_vat = R_ACT.alloc([9, 64], BF16, "vat")
    qTt = [R_ACT.alloc([TPC], BF16, "qT%d" % i) for i in range(2)]
    rt1, B_rt1 = R_ACT.alloc([XW], F32, "rt1")
    rt2, B_rt2 = R_ACT.alloc([XW], F32, "rt2")
    NS = 3
    sm_t = [R_ACT.alloc([256], F32, "sm%d" % i) for i in range(NS)]
    p_t = [R_ACT.alloc([256], F32, "p%d" % i) for i in range(NS)]
    pn_t = [R_ACT.alloc([256], BF16, "pn%d" % i) for i in range(NS)]
    pnT_t = [R_ACT.alloc([256], BF16, "pnT%d" % i) for i in range(NS)]
    st_t = [R_ACT.alloc([8], F32, "st%d" % i) for i in range(NS)]
    ucount = {"n": 0}

    for g in range(4):
        wvq, wbq = wget("AQ_%d" % g, 16)
        wvk, wbk = wget("AKV_%d" % g, 16)
        for (c0, n, bk) in [(0, 512, 0), (512, 512, 1), (1024, 128, 2)]:
            proj_fm(bank(bk)[:, 0:n], [PB[bk]], wvk, wbk, 0, 128, c0, n)
            proj_fm(bank(3 + bk)[:, 0:n], [PB[3 + bk]], wvk, wbk, 128, 128, c0, n)
            OP("dve", "tensor_tensor", [PB[bk], B_cst], [B_rt1], out=rt1[:, c0:c0 + n], in0=bank(bk)[:, 0:n],
               in1=cosT[:, c0:c0 + n], op=ALU.mult)
            OP("dve", "tensor_tensor", [PB[3 + bk], B_cst], [B_rt2], out=rt2[:, c0:c0 + n], in0=bank(3 + bk)[:, 0:n],
               in1=sinT[:, c0:c0 + n], op=ALU.mult)
        OP("dve", "tensor_tensor", [B_rt1, B_rt2], [B_kT], out=kT, in0=rt1, in1=rt2, op=ALU.add)
        for tt in range(9):
            bk = 6 + (tt % 2)
            for kc in range(16):
                MM(bank(bk)[:, 0:64], xT[:, kc, tt * 128:(tt + 1) * 128], wvk[:, kc, 256:320], kc == 0, kc == 15,
                   [wbk, B_xT], [PB[bk]])
            OP("act", "copy", [PB[bk]], [B_vat], out=vat[:, tt, :], in_=bank(bk)[:, 0:64])
        for jt in range(2):
            qT, B_qT = qTt[jt]
            for half in range(2):
                proj_fm(bank(half), [PB[half]], wvq, wbq, jt * 128, 128, 128 + half * 512, 512)
                proj_fm(bank(2 + half), [PB[2 + half]], wvq, wbq, 256 + jt * 128, 128, 128 + half * 512, 512)
            OP("dve", "tensor_tensor", [PB[0], PB[1], B_cst], [B_rt1], out=rt1[:, 0:TPC], in0=bank(0, 2),
               in1=cosT[:, 128:XW], op=ALU.mult)
            OP("dve", "tensor_tensor", [PB[2], PB[3], B_cst], [B_rt2], out=rt2[:, 0:TPC], in0=bank(2, 2),
               in1=sinT[:, 128:XW], op=ALU.mult)
            OP("dve", "tensor_tensor", [B_rt1, B_rt2], [B_qT], out=qT, in0=rt1[:, 0:TPC], in1=rt2[:, 0:TPC], op=ALU.add)
        wload_next()
        wload_next()
        for jt in range(2):
            qT, B_qT = qTt[jt]
            units = [(hh, n) for hh in range(2) for n in range(8)]
            po = 5

            def stageA(u, k):
                hh, n = u
                base = hh * 64
                head = g * 4 + jt * 2 + hh
                sb_ = k % 3
                sm, Bsm = sm_t[k % NS]
                p_, Bp = p_t[k % NS]
                pn, Bpn = pn_t[k % NS]
                st, Bst = st_t[k % NS]
                MM(bank(sb_)[:, 0:256], qT[base:base + 64, n * 128:(n + 1) * 128], kT[base:base + 64, n * 128:n * 128 + 256],
                   True, True, [B_qT, B_kT], [PB[sb_]])
                mk = cst[:, K_M0:K_M0 + 256] if n == 0 else cst[:, K_M1:K_M1 + 256]
                OP("dve", "scalar_tensor_tensor", [PB[sb_], B_cst], [Bsm], out=sm, in0=bank(sb_)[:, 0:256], scalar=0.125,
                   in1=mk, op0=ALU.mult, op1=ALU.add)
                OP("dve", "reduce_max", [Bsm], [Bst], out=st[:, 0:1], in_=sm, axis=AX.X)
                OP("dve", "tensor_scalar", [Bst, B_cst], [Bst], out=st[:, 1:2], in0=st[:, 0:1],
                   scalar1=sinks[:, head:head + 1], scalar2=-1.0, op0=ALU.max, op1=ALU.mult)
                ACT(p_, sm, AF.Exp, [Bsm, Bst], [Bp, Bst], bias=st[:, 1:2], scale=1.0, accum_out=st[:, 2:3])
                ACT(st[:, 3:4], sinks[:, head:head + 1], AF.Exp, [Bst, B_cst], [Bst], bias=st[:, 1:2], scale=1.0)
                OP("dve", "tensor_tensor", [Bst], [Bst], out=st[:, 4:5], in0=st[:, 2:3], in1=st[:, 3:4], op=ALU.add)
                OP("dve", "reciprocal", [Bst], [Bst], out=st[:, 5:6], in_=st[:, 4:5])
                OP("dve", "tensor_scalar", [Bp, Bst], [Bpn], out=pn, in0=p_, scalar1=st[:, 5:6], scalar2=None,
                   op0=ALU.mult)

            def stageB(u, k):
                pn, Bpn = pn_t[k % NS]
                pnT, BpnT = pnT_t[k % NS]
                tbk = 3 + (k % 2)
                tb = bank_bf(tbk)
                TR(tb[:, 0:128], pn[:, 0:128], ident, [Bpn, B_id], [PB[tbk]])
                TR(tb[:, 128:256], pn[:, 128:256], ident, [Bpn, B_id], [PB[tbk]])
                OP("act", "copy", [PB[tbk]], [BpnT], out=pnT, in_=tb[:, 0:256])

            def stageC(u, k):
                hh, n = u
                base = hh * 64
                pnT, BpnT = pnT_t[k % NS]
                oap = psum[base:base + 64, po * 512 + n * 128:po * 512 + (n + 1) * 128]
                pbs = [PB[po + (n // 4)]]
                MM(oap, vat[:, n, :], pnT[:, 0:128], True, False, [B_vat, BpnT], pbs)
                MM(oap, vat[:, n + 1, :], pnT[:, 128:256], False, True, [B_vat, BpnT], pbs)

            k0 = ucount["n"]
            nu = len(units)
            for i in range(nu + 2):
                if i < nu:
                    stageA(units[i], k0 + i)
                if 0 <= i - 1 < nu:
                    stageB(units[i - 1], k0 + i - 1)
                if 0 <= i - 2 < nu:
                    stageC(units[i - 2], k0 + i - 2)
            ucount["n"] = k0 + nu
            OP("act", "copy", [PB[po], PB[po + 1]], [B_oa], out=o_attT[:, g * 2 + jt, :], in_=bank(po, 2))
    if debug:
        finals.append(P.dma("sp", dbg["oatt"], o_attT.rearrange("p a b -> p (a b)"), reads=[B_oa], writes=[Buf("d")]))
    P.barrier()
    if stop_after == "ATT":
        st = P.emit(finals)
        es.close()
        return nc, st

    R_ACT.reset()
    R_X.top = x_mark

    tl_a = hgrn_tiles(R_ACT)
    qs_a = R_ACT.alloc([512], F32, "qs")
    sg_a = R_ACT.alloc([512], BF16, "sg")
    Z, BZ = R_ACT.alloc([4, 512], F32, "Z")
    e0, Be0 = R_ACT.alloc([512], F32, "e0")
    ref16, Bref = R_ACT.alloc([16], F32, "ref16")
    ops_t = {nm: R_X.alloc([512], BF16, nm) for nm in ["qt", "q128", "k128", "q64", "k64", "q32", "k32"]}
    scT, BscT = R_X.alloc([4, 128], BF16, "scT")
    Sbf, BSbf = R_X.alloc([4, 128], BF16, "Sbf")
    osq, Bosq = R_X.alloc([512], BF16, "osq")
    rstd, Brstd = R_X.alloc([512], F32, "rstd")
    t1, Bt1 = R_X.alloc([512], F32, "t1")
    OP("dve", "memset", [], [ops_t["q128"][1]], ops_t["q128"][0], 0.0)
    OP("dve", "memset", [], [ops_t["k128"][1]], ops_t["k128"][0], 0.0)

    def v3(ap, nblk):
        return ap.rearrange("p (a b) -> p a b", a=nblk)


    def alloc_any(shape, dt, name):
        n = 1
        for s_ in shape:
            n *= s_
        words = ((n if dt == F32 else (n + 1) // 2) + 7) // 8 * 8
        for R in (R_ACT, R_X, R_BIG):
            if R.top + words <= R.hi:
                return R.alloc(shape, dt, name)
        raise AssertionError("no room for " + name)

    tl_b = {}
    for nm, shp, dt in [("f", [512], F32), ("lf", [512], F32), ("b", [512], F32), ("ed", [512], BF16),
                        ("ebl", [8], F32), ("kdec", [512], BF16), ("kdT", [4, 128], BF16),
                        ("vfm", [512], BF16), ("vtok", [4, 128], BF16)]:
        tl_b[nm] = alloc_any(shp, dt, nm + "_b")
    qs_b = alloc_any([512], F32, "qs_b")
    sg_b = alloc_any([512], BF16, "sg_b")
    tl2, qs2, sg2 = [tl_a, tl_b], [qs_a, qs_b], [sg_a, sg_b]
    units2 = [(h, half) for h in range(8) for half in range(2)]
    wv_h = {}

    def p2_A(u):
        h, half = units2[u]
        if half == 0:
            wv_h[h] = wget("P2_%d" % h, 16)
        wv, wb = wv_h[h]
        x0 = 128 + half * 512
        hgrn_proj(tl2[u % 2], h, half, wv, wb, 128, 384, 2, 3)
        proj_fm(bank(0), [PB[0]], wv, wb, 0, 128, x0, 512)
        proj_fm(bank(1), [PB[1]], wv, wb, 256, 128, x0, 512)
        if half == 1:
            wload_next()

    def p2_CH(u):
        h, half = units2[u]
        hgrn_chain(tl2[u % 2], h, half, 2, 3)
        ACT(qs2[u % 2][0], bank(0), AF.Silu, [PB[0]], [qs2[u % 2][1]])
        ACT(sg2[u % 2][0], bank(1), AF.Silu, [PB[1]], [sg2[u % 2][1]])

    def p2_S2(u):
        if True:
            h, half = units2[u]
            tl = tl2[u % 2]
            qs, Bqs = qs2[u % 2]
            sg, Bsg = sg2[u % 2]
            pbq, pbsc, pbo = 4, 6, 7
            hgrn_s2(tl, 4, 5)
            pb_s = 5
            f, Bf = tl["f"]
            lf, Blf = tl["lf"]
            b, Bb = tl["b"]
            ebl, Bebl = tl["ebl"]
            vtok, Bvt = tl["vtok"]
            b4, z1 = v3(b, 4), v3(Z[:, 0, :], 4)
            OP("dve", "tensor_tensor", [Bb], [BZ], out=z1[:, :, 64:128], in0=b4[:, :, 64:128],
               in1=b4[:, :, 63:64].broadcast_to([128, 4, 64]), op=ALU.subtract)
            OP("dve", "tensor_tensor", [Bb], [BZ], out=z1[:, :, 0:64], in0=b4[:, :, 63:64].broadcast_to([128, 4, 64]),
               in1=b4[:, :, 0:64], op=ALU.subtract)
            b8, z2 = v3(b, 8), v3(Z[:, 1, :], 8)
            OP("dve", "tensor_tensor", [Bb], [BZ], out=z2[:, :, 32:64], in0=b8[:, :, 32:64],
               in1=b8[:, :, 31:32].broadcast_to([128, 8, 32]), op=ALU.subtract)
            OP("dve", "tensor_tensor", [Bb], [BZ], out=z2[:, :, 0:32], in0=b8[:, :, 31:32].broadcast_to([128, 8, 32]),
               in1=b8[:, :, 0:32], op=ALU.subtract)
            b16, lf16, z3, z4 = v3(b, 16), v3(lf, 16), v3(Z[:, 2, :], 16), v3(Z[:, 3, :], 16)
            OP("dve", "tensor_tensor", [Bb, Blf], [Bref], out=ref16, in0=b16[:, :, 0], in1=lf16[:, :, 0], op=ALU.subtract)
            rbc = ref16.unsqueeze(2).broadcast_to([128, 16, 32])
            OP("dve", "tensor_tensor", [Bb, Bref], [BZ], out=z3, in0=b16, in1=rbc, op=ALU.subtract)
            OP("dve", "tensor_tensor", [Bb, Bref], [BZ], out=z4, in0=rbc, in1=b16, op=ALU.subtract)
            Zf = Z.rearrange("p a b -> p (a b)")
            ACT(Zf, Zf, AF.Exp, [BZ], [BZ])
            ACT(e0, b, AF.Exp, [Bb], [Be0])
            def prod(nm, a, Ba, bb, Bbb, lo=0, hi=128, nblk=4):
                ap, B_ = ops_t[nm]
                if (lo, hi) == (0, 128):
                    OP("dve", "tensor_tensor", [Ba, Bbb], [B_], out=ap, in0=a, in1=bb, op=ALU.mult)
                else:
                    OP("dve", "tensor_tensor", [Ba, Bbb], [B_], out=v3(ap, nblk)[:, :, lo:hi], in0=v3(a, nblk)[:, :, lo:hi],
                       in1=v3(bb, nblk)[:, :, lo:hi], op=ALU.mult)
            prod("qt", qs, Bqs, e0, Be0)
            prod("q128", qs, Bqs, Z[:, 0, :], BZ, 64, 128)
            prod("k128", f, Bf, Z[:, 0, :], BZ, 0, 64)
            prod("q64", qs, Bqs, Z[:, 1, :], BZ)
            prod("k64", f, Bf, Z[:, 1, :], BZ)
            prod("q32", qs, Bqs, Z[:, 2, :], BZ)
            prod("k32", f, Bf, Z[:, 3, :], BZ)
            qt, q128, k128 = ops_t["qt"], ops_t["q128"], ops_t["k128"]
            q64, k64, q32, k32 = ops_t["q64"], ops_t["k64"], ops_t["q32"], ops_t["k32"]
            sc = bank(pbsc)
            MM(sc, zeros_bf, q128[0], True, False, [B_zeros, q128[1]], [PB[pbsc]])
            for c in range(4):
                MM(sc[:, c * 128:(c + 1) * 128], k128[0][:, c * 128:(c + 1) * 128], q128[0][:, c * 128:(c + 1) * 128],
                   False, False, [k128[1], q128[1]], [PB[pbsc]])
            for c in range(4):
                for blk in range(2):
                    o0 = c * 128 + blk * 64
                    MM(psum[blk * 64:blk * 64 + 32, pbsc * 512 + o0 + 32:pbsc * 512 + o0 + 64], k64[0][:, o0:o0 + 32],
                       q64[0][:, o0 + 32:o0 + 64], False, False, [k64[1], q64[1]], [PB[pbsc]], tile_position=(0, blk * 64))
                for blk in range(4):
                    o0 = c * 128 + blk * 32
                    MM(psum[blk * 32:blk * 32 + 32, pbsc * 512 + o0:pbsc * 512 + o0 + 32], k32[0][:, o0:o0 + 32],
                       q32[0][:, o0:o0 + 32], False, False, [k32[1], q32[1]], [PB[pbsc]],
                       tile_position=(0, blk * 32))
            MM(sc, zeros_bf, q128[0], False, True, [B_zeros, q128[1]], [PB[pbsc]])
            OP("dve", "tensor_tensor", [PB[pbsc], B_cst], [BscT], out=scT, in0=v3(sc, 4),
               in1=cmask.unsqueeze(1).broadcast_to([128, 4, 128]), op=ALU.mult)
            for c in range(4):
                OP("dve", "tensor_copy", [B_T], [BSbf], out=Sbf[:, c, :], in_=T[:, h, :])
                OP("dve", "scalar_tensor_tensor", [B_T, Bebl, PB[pb_s]], [B_T], out=T[:, h, :], in0=T[:, h, :],
                   scalar=ebl[:, half * 4 + c:half * 4 + c + 1], in1=bank(pb_s)[:, c * 128:(c + 1) * 128],
                   op0=ALU.mult, op1=ALU.add)
            ob = bank(pbo)
            for c in range(4):
                MM(ob[:, c * 128:(c + 1) * 128], Sbf[:, c, :], qt[0][:, c * 128:(c + 1) * 128], True, False,
                   [BSbf, qt[1]], [PB[pbo]])
                MM(ob[:, c * 128:(c + 1) * 128], vtok[:, c, :], scT[:, c, :], False, True, [Bvt, BscT], [PB[pbo]])
            ACT(osq, ob, AF.Square, [PB[pbo]], [Bosq])
            MM(bank(pbq), ones_bf, osq, True, True, [B_ones, Bosq], [PB[pbq]])
            ACT(rstd, bank(pbq), AF.Ln, [PB[pbq], B_vec], [Brstd], bias=eps_rms, scale=1.0 / 128.0)
            ACT(rstd, rstd, AF.Exp, [Brstd], [Brstd], scale=-0.5)
            OP("dve", "scalar_tensor_tensor", [PB[pbo], B_cst, Brstd], [Bt1], out=t1, in0=ob, scalar=gain[:, h:h + 1],
               in1=rstd, op0=ALU.mult, op1=ALU.mult)
            OP("dve", "tensor_tensor", [Bt1, Bsg], [B_oh], out=o_hT[:, h, half * 512:(half + 1) * 512], in0=t1, in1=sg,
               op=ALU.mult)

    p2_A(0)
    p2_CH(0)
    for u in range(16):
        if u + 1 < 16:
            p2_A(u + 1)
        p2_S2(u)
        if u + 1 < 16:
            p2_CH(u + 1)
    if debug:
        finals.append(P.dma("sp", dbg["oh"], o_hT.rearrange("p a b -> p (a b)"), reads=[B_oh], writes=[Buf("d")]))
    P.barrier()
    if stop_after == "A":
        st = P.emit(finals)
        es.close()
        return nc, st

    R_ACT.reset()
    R_X.top = x_mark
    mixedT, B_mx = R_ACT.alloc([16, TPC], BF16, "mixedT")
    sga, Bsga = R_X.alloc([TPC], F32, "sga")
    sgb, Bsgb = R_X.alloc([TPC], F32, "sgb")
    mtmp, Bmt = R_X.alloc([TPC], F32, "mtmp")
    for i in range(8):
        wvg, wbg = wget("G_%d" % i, 16)
        wvab, wbab = wget("AB_%d" % i, 16)
        for j in range(2):
            dc = 2 * i + j
            for half in range(2):
                proj_fm(bank(half), [PB[half]], wvg, wbg, j * 128, 128, 128 + half * 512, 512)
            ACT(sga, bank(0, 2), AF.Sigmoid, [PB[0], PB[1]], [Bsga])
            for half in range(2):
                proj_fm(bank(2 + half), [PB[2 + half]], wvg, wbg, 256 + j * 128, 128, 128 + half * 512, 512)
            ACT(sgb, bank(2, 2), AF.Sigmoid, [PB[2], PB[3]], [Bsgb])
            for half in range(2):
                for kc in range(8):
                    MM(bank(4 + half), wvab[:, kc, j * 128:(j + 1) * 128], o_hT[:, kc, half * 512:(half + 1) * 512],
                       kc == 0, kc == 7, [wbab, B_oh], [PB[4 + half]])
            OP("dve", "tensor_tensor", [PB[4], PB[5], Bsga], [Bmt], out=mtmp, in0=bank(4, 2), in1=sga, op=ALU.mult)
            for half in range(2):
                for kc in range(8):
                    MM(bank(6 + half), wvab[:, 8 + kc, j * 128:(j + 1) * 128], o_attT[:, kc, half * 512:(half + 1) * 512],
                       kc == 0, kc == 7, [wbab, B_oa], [PB[6 + half]])
            OP("dve", "tensor_tensor", [PB[6], PB[7], Bsgb], [Bsgb], out=sgb, in0=bank(6, 2), in1=sgb, op=ALU.mult)
            OP("dve", "tensor_tensor", [Bsgb, Bmt], [B_mx], out=mixedT[:, dc, :], in0=sgb, in1=mtmp, op=ALU.add)
        wload_next()
        wload_next()
    if debug:
        finals.append(P.dma("sp", dbg["mixed"], mixedT.rearrange("p a b -> p (a b)"), reads=[B_mx], writes=[Buf("d")]))
    P.barrier()
    if stop_after == "B":
        st = P.emit(finals)
        es.close()
        return nc, st

    R_BIG.reset()
    R_X.reset()
    xres, B_xr = R_BIG.alloc([8, D], F32, "xres")
    B_xrt = [Buf("xres%d" % t_) for t_ in range(8)]
    lng, B_lng = R_X.alloc([D], F32, "lng")
    lnb, B_lnb = R_X.alloc([D], F32, "lnb")
    for t_ in range(8):
        P.dma("sp", xres[:, t_, :], xtok_d[:, t_ * D:(t_ + 1) * D], writes=[B_xrt[t_]])
    P.dma("sp", lng, lnp_d[0:1, :].broadcast_to([128, D]), writes=[B_lng])
    P.dma("sp", lnb, lnp_d[1:2, :].broadcast_to([128, D]), writes=[B_lnb])
    bk_rr = {"n": 0}
    for cb in range(4):
        wv, wb = wget("WO_%d" % cb, 16)
        for t_ in range(8):
            bk = bk_rr["n"] % 4
            bk_rr["n"] += 1
            for kc in range(16):
                MM(bank(bk), mixedT[:, kc, t_ * 128:(t_ + 1) * 128], wv[:, kc, :], kc == 0, kc == 15, [B_mx, wb], [PB[bk]])
            OP("dve", "scalar_tensor_tensor", [B_xrt[t_], PB[bk]], [B_xrt[t_]], out=xres[:, t_, cb * 512:(cb + 1) * 512],
               in0=xres[:, t_, cb * 512:(cb + 1) * 512], scalar=ALPHA, in1=bank(bk), op0=ALU.mult, op1=ALU.add)
        wload_next()
    P.barrier()
    R_ACT.reset()
    x1T, B_x1T = R_ACT.alloc([16, TPC], BF16, "x1T")
    stats, Bstat = R_X.alloc([2, 32], F32, "stats")
    xbf = [R_X.alloc([D], BF16, "xbf%d" % i) for i in range(2)]

    def layer_norm(t_, gtile, Bg_, btile, Bb_, k):
        x = xres[:, t_, :]
        Bx = B_xrt[t_]
        st = stats[:, k % 2, :]
        for c4 in range(4):
            OP("dve", "bn_stats", [Bx], [Bstat], out=st[:, c4 * 6:(c4 + 1) * 6], in_=x[:, c4 * 512:(c4 + 1) * 512])
        OP("dve", "bn_aggr", [Bstat], [Bstat], out=st[:, 24:26], in_=st[:, 0:24])
        ACT(st[:, 26:27], st[:, 25:26], AF.Sqrt, [Bstat, B_vec], [Bstat], bias=eps_ln, scale=1.0)
        OP("dve", "reciprocal", [Bstat], [Bstat], out=st[:, 27:28], in_=st[:, 26:27])
        OP("dve", "tensor_scalar", [Bx, Bstat], [Bx], out=x, in0=x, scalar1=st[:, 24:25], scalar2=st[:, 27:28],
           op0=ALU.subtract, op1=ALU.mult)
        OP("dve", "tensor_tensor", [Bx, Bg_], [Bx], out=x, in0=x, in1=gtile, op=ALU.mult)
        OP("dve", "tensor_tensor", [Bx, Bb_], [Bx], out=x, in0=x, in1=btile, op=ALU.add)

    for t_ in range(8):
        layer_norm(t_, lng, B_lng, lnb, B_lnb, t_)
        xb, Bxb = xbf[t_ % 2]
        OP("act", "copy", [B_xrt[t_]], [Bxb], out=xb, in_=xres[:, t_, :])
        for hb in range(2):
            tbk = 4 + ((2 * t_ + hb) % 2)
            tb = bank_bf(tbk)
            for kk_ in range(8):
                kc = hb * 8 + kk_
                TR(tb[:, kk_ * 128:(kk_ + 1) * 128], xb[:, kc * 128:(kc + 1) * 128], ident, [Bxb, B_id], [PB[tbk]])
            OP("act" if hb == 0 else "dve", "copy" if hb == 0 else "tensor_copy", [PB[tbk]], [B_x1T],
               out=x1T[:, hb * 8:(hb + 1) * 8, t_ * 128:(t_ + 1) * 128], in_=tb.rearrange("p (a b) -> p a b", a=8))
        if debug:
            finals.append(P.dma("sp", dbg["x1"][:, t_ * D:(t_ + 1) * D], xres[:, t_, :], reads=[B_xrt[t_]], writes=[Buf("d")]))
        OP("act", "mul", [B_xrt[t_]], [B_xrt[t_]], out=xres[:, t_, :], in_=xres[:, t_, :], mul=ALPHA)
    P.barrier()
    if stop_after == "C":
        st = P.emit(finals)
        es.close()
        return nc, st

    R_X.reset()
    lng2, B_lng2 = R_X.alloc([D], F32, "lng2")
    lnb2, B_lnb2 = R_X.alloc([D], F32, "lnb2")
    P.dma("sp", lng2, lnp_d[2:3, :].broadcast_to([128, D]), writes=[B_lng2])
    P.dma("sp", lnb2, lnp_d[3:4, :].broadcast_to([128, D]), writes=[B_lnb2])
    hT = [R_X.alloc([4, TPC], BF16, "hT%d" % i) for i in range(2)]
    rl = [R_X.alloc([512], F32, "rl%d" % i) for i in range(2)]
    stats, Bstat = R_X.alloc([2, 32], F32, "stats2")
    hb_rr = {"n": 0}
    yb_rr = {"n": 0}
    for fb in range(16):
        w1v, w1b = wget("F1_%d" % fb, 16)
        w2v, w2b = wget("F2_%d" % fb, 4)
        h_, Bh = hT[fb % 2]
        for fc in range(4):
            for half in range(2):
                bk = hb_rr["n"] % 4
                hb_rr["n"] += 1
                r_, Br = rl[hb_rr["n"] % 2]
                for kc in range(16):
                    MM(bank(bk), w1v[:, kc, fc * 128:(fc + 1) * 128], x1T[:, kc, half * 512:(half + 1) * 512], kc == 0,
                       kc == 15, [w1b, B_x1T], [PB[bk]])
                ACT(r_, bank(bk), AF.Relu, [PB[bk]], [Br])
                ACT(h_[:, fc, half * 512:(half + 1) * 512], r_, AF.Square, [Br], [Bh])
        for t_ in range(8):
            for cb in range(4):
                bk = 4 + yb_rr["n"] % 4
                yb_rr["n"] += 1
                for fc in range(4):
                    MM(bank(bk), h_[:, fc, t_ * 128:(t_ + 1) * 128], w2v[:, fc, cb * 512:(cb + 1) * 512], fc == 0, fc == 3,
                       [Bh, w2b], [PB[bk]])
                OP("dve", "tensor_tensor", [B_xrt[t_], PB[bk]], [B_xrt[t_]], out=xres[:, t_, cb * 512:(cb + 1) * 512],
                   in0=xres[:, t_, cb * 512:(cb + 1) * 512], in1=bank(bk), op=ALU.add)
        wload_next()
        wload_next()
    for t_ in range(8):
        layer_norm(t_, lng2, B_lng2, lnb2, B_lnb2, t_)
        finals.append(P.dma("sp", out_d[t_ * 128:(t_ + 1) * 128, :], xres[:, t_, :], reads=[B_xrt[t_]], writes=[Buf("o")]))
    st = P.emit(finals)
    es.close()
    return nc, st


def make_in_maps(x, w_in, hg_lb_logits, hg_norm_gain, attn_sinks, w_branch_a, w_branch_b, w_out, ln1_gain, ln1_bias,
                 w_ff1, w_ff2, ln2_gain, ln2_bias):
    f32 = np.float32
    x2 = np.asarray(x, f32).reshape(S_TOT, D)
    wflat = _build_wflat(np.asarray(w_in, f32)[0], np.asarray(w_branch_a, f32)[0], np.asarray(w_branch_b, f32)[0],
                         np.asarray(w_out, f32)[0], np.asarray(w_ff1, f32)[0], np.asarray(w_ff2, f32)[0])
    wfa = np.ascontiguousarray(wflat[:, :W_SPLIT])
    wfb = np.ascontiguousarray(wflat[:, W_SPLIT:])
    del wflat
    lnp = np.stack([np.asarray(ln1_gain, f32)[0], np.asarray(ln1_bias, f32)[0], np.asarray(ln2_gain, f32)[0],
                    np.asarray(ln2_bias, f32)[0]]).astype(f32)
    lbl = np.asarray(hg_lb_logits, f32)
    gn = np.asarray(hg_norm_gain, f32)[0]
    sk = np.asarray(attn_sinks, f32)[0]
    maps = []
    for c in range(NCORES):
        t0 = c * TPC
        xe = np.zeros((XW, D), f32)
        if c > 0:
            xe[:128] = x2[t0 - 128:t0]
        xe[128:] = x2[t0:t0 + TPC]
        xT = np.ascontiguousarray(xe.T.reshape(16, 128, XW).transpose(1, 0, 2)).reshape(128, 16 * XW)
        xtok = np.ascontiguousarray(x2[t0:t0 + TPC].reshape(8, 128, D).transpose(1, 0, 2)).reshape(128, 8 * D)
        xp = np.zeros((7 * TPC, D), f32)
        if c > 0:
            xp[(7 - c) * TPC:] = x2[:c * TPC]
        xpre = np.ascontiguousarray(xp.reshape(14, 512, 16, 128).transpose(3, 0, 2, 1)).reshape(128, 14 * 16 * 512)
        maps.append({"xT": xT, "xtok": xtok, "xpre": xpre, "wflat": wfa, "wflat2": wfb, "cst": _build_consts(c, lbl, gn, sk), "lnp": lnp})
    return maps


_CACHE = {}


def kernel(**inputs):
    if "nc" not in _CACHE:
        _CACHE["nc"] = build_program()[0]
    nc = _CACHE["nc"]
    maps = make_in_maps(**inputs)
    res = run_bass_kernel_spmd(nc, maps, core_ids=list(range(NCORES)))
    out = np.concatenate([np.asarray(r["out"], np.float32) for r in res.results], axis=0)
    return out.reshape(1, S_TOT, D)
```

```python
import numpy as np
import concourse.bass as bass
import concourse.mybir as mybir
from concourse.bass_utils import run_bass_kernel_spmd
from contextlib import ExitStack

F32 = mybir.dt.float32
BF16 = mybir.dt.bfloat16
AF = mybir.ActivationFunctionType
ALU = mybir.AluOpType
AX = mybir.AxisListType

NCORES = 8
S_TOT = 8192
TPC = 1024
D = 2048
DFF = 8192
HGW = 1024
ALPHA = 2.0 ** 0.25
LN_EPS = 1e-5
RMS_EPS = 1e-6
NEG = -30000.0
XW = TPC + 128

C_HQ, C_HF, C_HI, C_HG, C_AQ, C_AK, C_AV, C_GA, C_GB = 0, 1024, 2048, 3072, 4096, 5120, 5376, 5632, 7680

K_ID, K_CM, K_RS, K_M0, K_M1, K_COS, K_SIN = 0, 128, 256, 768, 1024, 1280, 2432
K_RM, K_L0, K_L1, K_GN, K_SK = 3584, 3592, 3600, 3608, 3616
NCST = 3632


class Buf:
    __slots__ = ("name", "last_w", "readers", "excl")

    def __init__(self, name, excl=False):
        self.name = name
        self.last_w = None
        self.readers = []
        self.excl = excl


class Op:
    __slots__ = ("eng", "fn", "deps", "is_dma", "sig", "sem", "cnt", "idx", "waits", "inc", "sw", "retire_of")

    def __init__(self, eng, fn, is_dma):
        self.eng = eng
        self.fn = fn
        self.deps = []
        self.is_dma = is_dma
        self.sig = False
        self.sem = None
        self.cnt = 0
        self.waits = []
        self.inc = 1
        self.sw = False
        self.retire_of = None


ENGS = ("pe", "act", "dve", "pool", "sp")


class Prog:
    def __init__(self, nc):
        self.nc = nc
        self.ops = []
        self.bar = None
        self.last = {}
        self.dmas = []
        self.sw_pending = None

    def op(self, eng, fn, reads=(), writes=(), is_dma=False):
        o = Op(eng, fn, is_dma)
        o.idx = len(self.ops)
        deps = set()
        ex = [b for b in reads if b.excl]
        if ex:
            reads = [b for b in reads if not b.excl]
            writes = list(writes) + [b for b in ex if b not in writes]
        for b in reads:
            if b.last_w is not None:
                deps.add(b.last_w)
        for b in writes:
            if b.last_w is not None:
                deps.add(b.last_w)
            for r in b.readers:
                deps.add(r)
        if self.bar is not None:
            deps.add(self.bar)
        o.deps = sorted(deps, key=lambda d: d.idx)
        for b in reads:
            b.readers.append(o)
        for b in writes:
            b.last_w = o
            b.readers = []
        self.ops.append(o)
        self.last[eng] = o
        if is_dma:
            self.dmas.append(o)
        return o

    def dma(self, q, out, in_, reads=(), writes=(), **kw):
        if q == "pool":
            return self.swdma(out, in_, reads, writes, **kw)
        return self.op(q, lambda e: e.dma_start(out=out, in_=in_, **kw), reads, writes, is_dma=True)

    def swdma(self, out, in_, reads=(), writes=(), **kw):
        o = self.op("pool", lambda e: e.dma_start(out=out, in_=in_, **kw), reads, writes, is_dma=True)
        o.sw = True
        return o

    def _retire(self, pend):
        d, writes = pend
        r = self.op("pool", None, [], writes)
        r.retire_of = d
        return r

    def sw_flush(self):
        if self.sw_pending is not None:
            self._retire(self.sw_pending)
            self.sw_pending = None

    def barrier(self):
        self.sw_flush()
        hub = Op("sp", lambda e: e.nop(), False)
        hub.idx = len(self.ops)
        deps = set(self.last.values()) | set(self.dmas)
        if self.bar is not None:
            deps.add(self.bar)
        hub.deps = sorted(deps, key=lambda d: d.idx)
        self.ops.append(hub)
        self.last = {"sp": hub}
        self.dmas = []
        self.bar = hub
        return hub

    def emit(self, final_wait_ops=()):
        nc = self.nc
        self.sw_flush()
        for o in self.ops:
            for d in o.deps:
                if d.eng == "pe" and o.eng == "pe" and not d.is_dma:
                    continue
                d.sig = True
        for o in final_wait_ops:
            o.sig = True
        eng_cnt = {e: 0 for e in ENGS}
        eng_sem = {e: nc.alloc_semaphore(name="es_" + e) for e in ENGS}
        NDMA = 4
        dma_pool = [nc.alloc_semaphore(name="ds%d" % i) for i in range(NDMA)]
        dma_cnt = [0] * NDMA
        dma_last = [None] * NDMA
        rr = 0
        swn = 0
        for o in self.ops:
            if o.sw:
                o.sig = True
                o.sem, o.cnt, o.inc = nc.alloc_semaphore(name="sw%d" % swn), 16, 16
                swn += 1
                continue
            if not o.sig:
                continue
            if o.is_dma:
                k = rr % NDMA
                rr += 1
                if dma_last[k] is not None:
                    o.deps.append(dma_last[k])
                dma_last[k] = o
                dma_cnt[k] += 16
                o.sem, o.cnt, o.inc = dma_pool[k], dma_cnt[k], 16
            else:
                eng_cnt[o.eng] += 1
                o.sem, o.cnt, o.inc = eng_sem[o.eng], eng_cnt[o.eng], 1
        waited = {e: {} for e in ENGS}
        for o in self.ops:
            w = waited[o.eng]
            need = {}
            for d in o.deps:
                if d.eng == "pe" and o.eng == "pe" and not d.is_dma:
                    continue
                key = id(d.sem)
                if w.get(key, 0) >= d.cnt:
                    continue
                if key not in need or need[key][1] < d.cnt:
                    need[key] = (d.sem, d.cnt)
            for key, (s, c) in need.items():
                w[key] = c
                o.waits.append((s, c))
        per_eng = {e: [o for o in self.ops if o.eng == e] for e in ENGS}
        finals = {}
        for o in final_wait_ops:
            prev = finals.get(id(o.sem), (None, 0))[1]
            finals[id(o.sem)] = (o.sem, max(o.cnt, prev))

        def run(name, eng):
            for o in per_eng[name]:
                for (s, c) in o.waits:
                    eng.wait_ge(s, c)
                if o.retire_of is not None:
                    d = o.retire_of
                    if not any(id(s) == id(d.sem) for (s, c) in o.waits):
                        eng.wait_ge(d.sem, 16)
                    eng.sem_clear(d.sem)
                    inst = eng.nop()
                else:
                    inst = o.fn(eng)
                if o.sig:
                    inst.then_inc(o.sem, o.inc)

        with nc.Block() as block:
            @block.tensor
            def _(e):
                run("pe", e)

            @block.scalar
            def _(e):
                run("act", e)

            @block.vector
            def _(e):
                run("dve", e)

            @block.gpsimd
            def _(e):
                run("pool", e)

            @block.sync
            def _(e):
                run("sp", e)
                for (s, c) in finals.values():
                    e.wait_ge(s, c)
        st = {e: len(per_eng[e]) for e in ENGS}
        st["waits"] = sum(len(o.waits) for o in self.ops)
        st["sem_max"] = dict(eng_cnt)
        return st


def _weight_blocks():
    blks = []
    for h in range(8):
        blks.append(("P1_%d" % h, 4096))
    for g in range(4):
        blks.append(("AQ_%d" % g, 8192))
        blks.append(("AKV_%d" % g, 5120))
    for h in range(8):
        blks.append(("P2_%d" % h, 8192))
    for i in range(8):
        blks.append(("G_%d" % i, 8192))
        blks.append(("AB_%d" % i, 4096))
    for cb in range(4):
        blks.append(("WO_%d" % cb, 8192))
    for fb in range(16):
        blks.append(("F1_%d" % fb, 8192))
        blks.append(("F2_%d" % fb, 8192))
    offs = {}
    o = 0
    order = []
    for name, n in blks:
        offs[name] = (o, n)
        order.append(name)
        o += n
    return offs, order, o


W_OFFS, W_ORDER, W_TOT = _weight_blocks()
W_SPLIT = W_OFFS["F1_0"][0]


def _tile_w(W, cols):
    K = W.shape[0]
    blk = W[:, cols]
    n = blk.shape[1]
    return np.ascontiguousarray(blk.reshape(K // 128, 128, n).transpose(1, 0, 2)).reshape(128, (K // 128) * n)


def _build_wflat(w_in, w_a, w_b, w_out, w1, w2):
    wf = np.empty((128, W_TOT), np.float32)
    ar = np.arange

    def put(name, arr):
        o, n = W_OFFS[name]
        assert arr.shape == (128, n), (name, arr.shape, n)
        wf[:, o:o + n] = arr

    def perm64(base):
        return np.concatenate([base + 32 + ar(32), base + ar(32)])

    for h in range(8):
        put("P1_%d" % h, _tile_w(w_in, np.concatenate([C_HF + h * 128 + ar(128), C_HI + h * 128 + ar(128)])))
        put("P2_%d" % h, _tile_w(w_in, np.concatenate([C_HQ + h * 128 + ar(128), C_HF + h * 128 + ar(128),
                                                        C_HG + h * 128 + ar(128), C_HI + h * 128 + ar(128)])))
    for g in range(4):
        qc = C_AQ + g * 256 + ar(256)
        qp = np.concatenate([perm64(C_AQ + g * 256 + hh * 64) for hh in range(4)])
        put("AQ_%d" % g, _tile_w(w_in, np.concatenate([qc, qp])))
        kc = C_AK + g * 64 + ar(64)
        kp = perm64(C_AK + g * 64)
        vc = C_AV + g * 64 + ar(64)
        put("AKV_%d" % g, _tile_w(w_in, np.concatenate([kc, kc, kp, kp, vc])))
    for i in range(8):
        put("G_%d" % i, _tile_w(w_in, np.concatenate([C_GA + i * 256 + ar(256), C_GB + i * 256 + ar(256)])))
        ab = np.concatenate([_tile_w(w_a, i * 256 + ar(256)), _tile_w(w_b, i * 256 + ar(256))], axis=1)
        put("AB_%d" % i, ab)
    for cb in range(4):
        put("WO_%d" % cb, _tile_w(w_out, cb * 512 + ar(512)))
    for fb in range(16):
        put("F1_%d" % fb, _tile_w(w1, fb * 512 + ar(512)))
        put("F2_%d" % fb, _tile_w(w2[fb * 512:(fb + 1) * 512, :], ar(2048)))
    return wf


def _build_consts(core, lb_logits, gain, sinks):
    c = np.zeros((128, NCST), np.float32)
    p = np.arange(128)
    c[:, K_ID:K_ID + 128] = np.eye(128, dtype=np.float32)
    c[:, K_CM:K_CM + 128] = (p[:, None] <= p[None, :]).astype(np.float32)
    rs = np.ones(512, np.float32)
    rs[::128] = 0.0
    c[:, K_RS:K_RS + 512] = rs[None, :]
    q = p[:, None]
    s = np.arange(256)[None, :]
    valid = np.where(s < 128, s > q, (s - 128) <= q)
    m1 = np.where(valid, 0.0, NEG).astype(np.float32)
    m0 = m1.copy()
    if core == 0:
        m0[:, :128] = NEG
    c[:, K_M0:K_M0 + 256] = m0
    c[:, K_M1:K_M1 + 256] = m1
    d = p % 64
    j = d % 32
    inv = (np.float32(10000.0) ** (-np.arange(32, dtype=np.float32) / np.float32(32))).astype(np.float32)
    pos = (core * TPC - 128 + np.arange(XW)).astype(np.float32)
    ang = (pos[None, :] * inv[j][:, None]).astype(np.float32)
    c[:, K_COS:K_COS + XW] = np.cos(ang).astype(np.float32)
    sn = np.sin(ang).astype(np.float32)
    c[:, K_SIN:K_SIN + XW] = np.where((d < 32)[:, None], -sn, sn)
    c[:, K_RM:K_RM + 8] = (np.arange(8) < core).astype(np.float32)[None, :]
    c[:, K_L0:K_L0 + 8] = lb_logits[0].reshape(8, 128).T
    c[:, K_L1:K_L1 + 8] = lb_logits[1].reshape(8, 128).T
    c[:, K_GN:K_GN + 8] = gain.reshape(8, 128).T
    c[:, K_SK:K_SK + 16] = sinks.reshape(1, 16)
    return c


ARENA_W = 51200
NWB = 3


STOP_BLOCKS = {"P1": 8, "ATT": 16, "A": 24, "B": 40, "C": 44, None: len(W_ORDER)}


def wtot_for(stop_after):
    nb = STOP_BLOCKS[stop_after]
    o, n = W_OFFS[W_ORDER[nb - 1]]
    return o + n


def build_program(debug=False, stop_after=None):
    nc = bass.Bass("TRN2", target_bir_lowering=False)
    NBLK = STOP_BLOCKS[stop_after]

    def din(name, shape, dt=F32):
        return nc.dram_tensor(name, shape, dt, kind="ExternalInput").ap()

    xT_d = din("xT", [128, 16 * XW])
    xtok_d = din("xtok", [128, 8 * D])
    wflat_d = din("wflat", [128, min(wtot_for(stop_after), W_SPLIT)])
    wflat2_d = din("wflat2", [128, W_TOT - W_SPLIT]) if wtot_for(stop_after) > W_SPLIT else None
    cst_d = din("cst", [128, NCST])
    lnp_d = din("lnp", [4, D])
    out_d = nc.dram_tensor("out", [TPC, D], F32, kind="ExternalOutput").ap()
    xpre_d = din("xpre", [128, 14 * 16 * 512])
    dbg = {}
    if debug:
        dbg["oh"] = nc.dram_tensor("dbg_oh", [128, 8 * TPC], BF16, kind="ExternalOutput").ap()
        dbg["oatt"] = nc.dram_tensor("dbg_oatt", [128, 8 * TPC], BF16, kind="ExternalOutput").ap()
        dbg["mixed"] = nc.dram_tensor("dbg_mixed", [128, 16 * TPC], BF16, kind="ExternalOutput").ap()
        dbg["x1"] = nc.dram_tensor("dbg_x1", [128, 8 * D], F32, kind="ExternalOutput").ap()
        dbg["T"] = nc.dram_tensor("dbg_T", [128, 8 * 128], F32, kind="ExternalOutput").ap()

    P = Prog(nc)
    finals = []
    es = ExitStack()
    arena = es.enter_context(nc.sbuf_tensor("arena", [128, ARENA_W], F32))
    psum = es.enter_context(nc.psum_tensor("psum", [128, 8 * 512], F32))
    PB = [Buf("psb%d" % i, excl=True) for i in range(8)]

    def bank(i, n=1):
        return psum[:, i * 512:(i + n) * 512]

    def bank_bf(i):
        return psum[:, i * 512:(i + 1) * 512].bitcast(BF16)

    class Region:
        def __init__(self, lo, hi):
            self.lo, self.hi, self.top = lo, hi, lo

        def reset(self):
            self.top = self.lo

        def alloc(self, shape, dt=F32, name="t"):
            n = 1
            for s_ in shape:
                n *= s_
            words = n if dt == F32 else (n + 1) // 2
            words = (words + 7) // 8 * 8
            off = self.top
            self.top += words
            assert self.top <= self.hi, ("arena region overflow", name, self.top, self.hi)
            ap = arena[:, off:off + words]
            if dt != F32:
                ap = ap.bitcast(dt)
            ap = ap[:, 0:n]
            if len(shape) == 2:
                ap = ap.rearrange("p (a b) -> p a b", a=shape[0])
            elif len(shape) == 3:
                ap = ap.rearrange("p (a b c) -> p a b c", a=shape[0], b=shape[1])
            return ap, Buf(name)

    R_C = Region(0, 4096)
    R_W = Region(4096, 4096 + NWB * 4096)
    o_ = 4096 + NWB * 4096
    R_BIG = Region(o_, o_ + 16384)
    o_ += 16384
    R_ACT = Region(o_, o_ + 8192)
    o_ += 8192
    R_X = Region(o_, ARENA_W)
    assert ARENA_W - o_ >= 8192, (ARENA_W, o_)

    def OP(eng, method, reads, writes, *args, **kw):
        return P.op(eng, lambda e: getattr(e, method)(*args, **kw), reads, writes)

    def MM(out, lhsT, rhs, start, stop, reads, writes, **kw):
        return P.op("pe", lambda e: e.matmul(out, lhsT=lhsT, rhs=rhs, start=start, stop=stop, **kw), reads, writes)

    def TR(out, in_, ident, reads, writes):
        return P.op("pe", lambda e: e.transpose(out, in_, ident), reads, writes)

    def ACT(out, in_, func, reads, writes, **kw):
        return P.op("act", lambda e: e.activation(out=out, in_=in_, func=func, **kw), reads, writes)

    cst, B_cst = R_C.alloc([NCST], F32, "cst")
    P.dma("sp", cst, cst_d, writes=[B_cst])
    ident, B_id = R_C.alloc([128], BF16, "ident")
    ones_bf, B_ones = R_C.alloc([128], BF16, "ones")
    zeros_bf, B_zeros = R_C.alloc([128], BF16, "zeros")
    vec, B_vec = R_C.alloc([32], F32, "vec")
    OP("act", "copy", [B_cst], [B_id], out=ident, in_=cst[:, K_ID:K_ID + 128])
    OP("dve", "memset", [], [B_ones], ones_bf, 1.0)
    OP("dve", "memset", [], [B_zeros], zeros_bf, 0.0)
    lb = vec[:, 0:8]
    oml = vec[:, 8:16]
    OP("dve", "tensor_tensor", [B_cst], [B_vec], out=lb, in0=cst[:, K_L0:K_L0 + 8], in1=cst[:, K_L1:K_L1 + 8], op=ALU.subtract)
    ACT(lb, lb, AF.Sigmoid, [B_vec], [B_vec])
    OP("dve", "tensor_scalar", [B_vec], [B_vec], out=oml, in0=lb, scalar1=-1.0, scalar2=1.0, op0=ALU.mult, op1=ALU.add)
    OP("dve", "memset", [], [B_vec], vec[:, 16:17], RMS_EPS)
    OP("dve", "memset", [], [B_vec], vec[:, 17:18], LN_EPS)
    eps_rms = vec[:, 16:17]
    eps_ln = vec[:, 17:18]
    cmask = cst[:, K_CM:K_CM + 128]
    restart = cst[:, K_RS:K_RS + 512]
    gain = cst[:, K_GN:K_GN + 8]
    sinks = cst[:, K_SK:K_SK + 16]
    rmask = cst[:, K_RM:K_RM + 8]

    wslots = [R_W.alloc([8192], BF16, "wslot%d" % i) for i in range(NWB)]
    wstate = {"next": 0}

    def wload_next():
        i = wstate["next"]
        if i >= NBLK:
            return
        wstate["next"] = i + 1
        name = W_ORDER[i]
        off, n = W_OFFS[name]
        ap, b = wslots[i % NWB]
        src = wflat_d[:, off:off + n] if off < W_SPLIT else wflat2_d[:, off - W_SPLIT:off - W_SPLIT + n]
        P.dma("pool", ap[:, 0:n], src, writes=[b], max_dma_last_dim=4096)

    wuse = {"i": 0}

    def wget(name, K):
        i = wuse["i"]
        assert W_ORDER[i] == name, (W_ORDER[i], name)
        wuse["i"] = i + 1
        off, n = W_OFFS[name]
        ap, b = wslots[i % NWB]
        return ap[:, 0:n].rearrange("p (k c) -> p k c", k=K), b

    xT, B_xT = R_BIG.alloc([16, XW], BF16, "xT")
    o_hT, B_oh = R_BIG.alloc([8, TPC], BF16, "o_hT")
    T, B_T = R_BIG.alloc([8, 128], F32, "T")

    def proj_fm(ps_ap, ps_bufs, wv, wb, col0, ncol_part, xcol0, ntok, xsrc=None, Bx=None):
        if xsrc is None:
            xsrc, Bx = xT, B_xT
        for kc in range(16):
            MM(ps_ap, wv[:, kc, col0:col0 + ncol_part], xsrc[:, kc, xcol0:xcol0 + ntok], kc == 0, kc == 15,
               [wb, Bx], ps_bufs)

    def hgrn_tiles(R):
        t = {}
        for nm, shp, dt in [("f", [512], F32), ("lf", [512], F32), ("b", [512], F32), ("ed", [512], BF16),
                            ("ebl", [8], F32), ("kdec", [512], BF16), ("kdT", [4, 128], BF16),
                            ("vfm", [512], BF16), ("vtok", [4, 128], BF16)]:
            t[nm] = R.alloc(shp, dt, nm)
        return t

    def hgrn_proj(t, h, half, wv, wb, col_f, col_i, pb_f, pb_i, xsrc=None, Bx=None, x0=None):
        if x0 is None:
            x0 = 128 + half * 512
        proj_fm(bank(pb_f), [PB[pb_f]], wv, wb, col_f, 128, x0, 512, xsrc, Bx)
        proj_fm(bank(pb_i), [PB[pb_i]], wv, wb, col_i, 128, x0, 512, xsrc, Bx)

    def hgrn_s1(t, h, half, wv, wb, col_f, col_i, pb_f, pb_i, xsrc=None, Bx=None, x0=None):
        hgrn_proj(t, h, half, wv, wb, col_f, col_i, pb_f, pb_i, xsrc, Bx, x0)
        hgrn_chain(t, h, half, pb_f, pb_i)

    def hgrn_chain(t, h, half, pb_f, pb_i):
        f, Bf = t["f"]
        lf, Blf = t["lf"]
        b, Bb = t["b"]
        ed, Bed = t["ed"]
        ebl, Bebl = t["ebl"]
        kdec, Bkd = t["kdec"]
        vfm, Bvfm = t["vfm"]
        ACT(f, bank(pb_f), AF.Sigmoid, [PB[pb_f]], [Bf])
        OP("dve", "tensor_scalar", [Bf, B_vec], [Bf], out=f, in0=f, scalar1=oml[:, h:h + 1], scalar2=lb[:, h:h + 1],
           op0=ALU.mult, op1=ALU.add)
        ACT(lf, f, AF.Ln, [Bf], [Blf])
        OP("act", "copy", [PB[pb_i]], [Bvfm], out=vfm, in_=bank(pb_i))
        OP("dve", "tensor_scalar", [Bf], [Bf], out=f, in0=f, scalar1=-1.0, scalar2=1.0, op0=ALU.mult, op1=ALU.add)
        OP("dve", "tensor_tensor_scan", [Blf, B_cst], [Bb], out=b, data0=restart, data1=lf, initial=0.0,
           op0=ALU.mult, op1=ALU.add)
        ACT(ebl[:, half * 4:half * 4 + 4], b[:, 127::128], AF.Exp, [Bb], [Bebl])
        for c in range(4):
            ACT(ed[:, c * 128:(c + 1) * 128], b[:, c * 128:(c + 1) * 128], AF.Exp, [Bb], [Bed], scale=-1.0,
                bias=b[:, c * 128 + 127:c * 128 + 128])
        OP("dve", "tensor_tensor", [Bf, Bed], [Bkd], out=kdec, in0=f, in1=ed, op=ALU.mult)

    def hgrn_s2(t, pb_t, pb_s):
        hgrn_s2b(t, pb_t)
        hgrn_s2c(t, pb_s)

    def hgrn_s2c(t, pb_s):
        kdT, BkdT = t["kdT"]
        vtok, Bvt = t["vtok"]
        for c in range(4):
            MM(bank(pb_s)[:, c * 128:(c + 1) * 128], kdT[:, c, :], vtok[:, c, :], True, True, [BkdT, Bvt], [PB[pb_s]])

    def hgrn_s2b(t, pb_t):
        kdec, Bkd = t["kdec"]
        kdT, BkdT = t["kdT"]
        vfm, Bvfm = t["vfm"]
        vtok, Bvt = t["vtok"]
        tb = bank_bf(pb_t)
        for c in range(4):
            TR(tb[:, c * 128:(c + 1) * 128], kdec[:, c * 128:(c + 1) * 128], ident, [Bkd, B_id], [PB[pb_t]])
        for c in range(4):
            TR(tb[:, 512 + c * 128:512 + (c + 1) * 128], vfm[:, c * 128:(c + 1) * 128], ident, [Bvfm, B_id], [PB[pb_t]])
        OP("act", "copy", [PB[pb_t]], [BkdT, Bvt], out=kdT.rearrange("p a b -> p (a b)"), in_=tb[:, 0:512])
        OP("dve", "tensor_copy", [PB[pb_t]], [Bvt], out=vtok.rearrange("p a b -> p (a b)"), in_=tb[:, 512:1024])

    def hgrn_fchain(t, h, half, wv, wb, col_f, col_i, pb_f, pb_i, pb_t, pb_s, xsrc=None, Bx=None, x0=None):
        hgrn_s1(t, h, half, wv, wb, col_f, col_i, pb_f, pb_i, xsrc, Bx, x0)
        hgrn_s2(t, pb_t, pb_s)

    R_ACT.reset()
    R_X.reset()
    OP("dve", "memset", [], [B_T], T, 0.0)
    xh = [R_ACT.alloc([16, 512], BF16, "xh%d" % i) for i in range(2)]
    p1x, B_p1x = R_X.alloc([8192], BF16, "p1x")
    tls = [hgrn_tiles(R_X), hgrn_tiles(R_X)]
    p1w = []
    for h2 in range(4):
        off, n = W_OFFS["P1_%d" % (2 * h2)]
        ap, b = wslots[h2] if h2 < 3 else (p1x, B_p1x)
        P.dma("pool", ap, wflat_d[:, off:off + 8192], writes=[b], max_dma_last_dim=4096)
        for j in range(2):
            p1w.append((ap[:, j * 4096:(j + 1) * 4096].rearrange("p (k c) -> p k c", k=16), b))
    wstate["next"] = 8
    wuse["i"] = 8

    def prefix_c(u):
        sh, h = divmod(u, 8)
        par = u % 2
        t = tls[par]
        pb_s = 3 + par * 4
        hgrn_s2c(t, pb_s)
        for c in range(4):
            OP("dve", "scalar_tensor_tensor", [B_T, t["ebl"][1], PB[pb_s]], [B_T], out=T[:, h, :],
               in0=T[:, h, :], scalar=t["ebl"][0][:, c:c + 1],
               in1=bank(pb_s)[:, c * 128:(c + 1) * 128], op0=ALU.mult, op1=ALU.add)

    NU = 14 * 8
    for u in range(NU + 2):
        if u < NU:
            sh, h = divmod(u, 8)
            xb_, Bxb_ = xh[sh % 2]
            if h == 0:
                P.dma("pool", xb_, xpre_d[:, sh * 8192:(sh + 1) * 8192].rearrange("p (k t) -> p k t", k=16), writes=[Bxb_],
                      max_dma_last_dim=4096)
                if sh == 1:
                    P.dma("pool", xT, xT_d.rearrange("p (k t) -> p k t", k=16), writes=[B_xT], max_dma_last_dim=4096)
            wv, wb = p1w[h]
            par = u % 2
            hgrn_proj(tls[par], h, 0, wv, wb, 0, 128, 0 + par * 4, 1 + par * 4, xb_, Bxb_, 0)
        if 0 <= u - 2 < NU:
            prefix_c(u - 2)
        if 0 <= u - 1 < NU:
            hgrn_s2b(tls[(u - 1) % 2], 2 + ((u - 1) % 2) * 4)
        if u < NU:
            hgrn_chain(tls[par], h, 0, 0 + par * 4, 1 + par * 4)
    if debug:
        finals.append(P.dma("sp", dbg["T"], T.rearrange("p a b -> p (a b)"), reads=[B_T], writes=[Buf("d")]))
    P.barrier()
    for _ in range(NWB):
        wload_next()
    R_ACT.reset()
    R_X.reset()
    o_attT, B_oa = R_X.alloc([8, TPC], BF16, "o_attT")
    x_mark = R_X.top
    if stop_after == "P1":
        st = P.emit(finals)
        es.close()
        return nc, st

    R_ACT.reset()
    cosT = cst[:, K_COS:K_COS + XW]
    sinT = cst[:, K_SIN:K_SIN + XW]
    kT, B_kT = R_ACT.alloc([XW], BF16, "kT")
    vat, B_vat = R_ACT.alloc([9, 64], BF16, "vat")
    qTt = [R_ACT.alloc([TPC], BF16, "qT%d" % i) for i in range(2)]
    rt1, B_rt1 = R_ACT.alloc([XW], F32, "rt1")
    rt2, B_rt2 = R_ACT.alloc([XW], F32, "rt2")
    NS = 3
    sm_t = [R_ACT.alloc([256], F32, "sm%d" % i) for i in range(NS)]
    p_t = [R_ACT.alloc([256], F32, "p%d" % i) for i in range(NS)]
    pn_t = [R_ACT.alloc([256], BF16, "pn%d" % i) for i in range(NS)]
    pnT_t = [R_ACT.alloc([256], BF16, "pnT%d" % i) for i in range(NS)]
    st_t = [R_ACT.alloc([8], F32, "st%d" % i) for i in range(NS)]
    ucount = {"n": 0}

    for g in range(4):
        wvq, wbq = wget("AQ_%d" % g, 16)
        wvk, wbk = wget("AKV_%d" % g, 16)
        for (c0, n, bk) in [(0, 512, 0), (512, 512, 1), (1024, 128, 2)]:
            proj_fm(bank(bk)[:, 0:n], [PB[bk]], wvk, wbk, 0, 128, c0, n)
            proj_fm(bank(3 + bk)[:, 0:n], [PB[3 + bk]], wvk, wbk, 128, 128, c0, n)
            OP("dve", "tensor_tensor", [PB[bk], B_cst], [B_rt1], out=rt1[:, c0:c0 + n], in0=bank(bk)[:, 0:n],
               in1=cosT[:, c0:c0 + n], op=ALU.mult)
            OP("dve", "tensor_tensor", [PB[3 + bk], B_cst], [B_rt2], out=rt2[:, c0:c0 + n], in0=bank(3 + bk)[:, 0:n],
               in1=sinT[:, c0:c0 + n], op=ALU.mult)
        OP("dve", "tensor_tensor", [B_rt1, B_rt2], [B_kT], out=kT, in0=rt1, in1=rt2, op=ALU.add)
        for tt in range(9):
            bk = 6 + (tt % 2)
            for kc in range(16):
                MM(bank(bk)[:, 0:64], xT[:, kc, tt * 128:(tt + 1) * 128], wvk[:, kc, 256:320], kc == 0, kc == 15,
                   [wbk, B_xT], [PB[bk]])
            OP("act", "copy", [PB[bk]], [B_vat], out=vat[:, tt, :], in_=bank(bk)[:, 0:64])
        for jt in range(2):
            qT, B_qT = qTt[jt]
            for half in range(2):
                proj_fm(bank(half), [PB[half]], wvq, wbq, jt * 128, 128, 128 + half * 512, 512)
                proj_fm(bank(2 + half), [PB[2 + half]], wvq, wbq, 256 + jt * 128, 128, 128 + half * 512, 512)
            OP("dve", "tensor_tensor", [PB[0], PB[1], B_cst], [B_rt1], out=rt1[:, 0:TPC], in0=bank(0, 2),
               in1=cosT[:, 128:XW], op=ALU.mult)
            OP("dve", "tensor_tensor", [PB[2], PB[3], B_cst], [B_rt2], out=rt2[:, 0:TPC], in0=bank(2, 2),
               in1=sinT[:, 128:XW], op=ALU.mult)
            OP("dve", "tensor_tensor", [B_rt1, B_rt2], [B_qT], out=qT, in0=rt1[:, 0:TPC], in1=rt2[:, 0:TPC], op=ALU.add)
        wload_next()
        wload_next()
        for jt in range(2):
            qT, B_qT = qTt[jt]
            units = [(hh, n) for hh in range(2) for n in range(8)]
            po = 5

            def stageA(u, k):
                hh, n = u
                base = hh * 64
                head = g * 4 + jt * 2 + hh
                sb_ = k % 3
                sm, Bsm = sm_t[k % NS]
                p_, Bp = p_t[k % NS]
                pn, Bpn = pn_t[k % NS]
                st, Bst = st_t[k % NS]
                MM(bank(sb_)[:, 0:256], qT[base:base + 64, n * 128:(n + 1) * 128], kT[base:base + 64, n * 128:n * 128 + 256],
                   True, True, [B_qT, B_kT], [PB[sb_]])
                mk = cst[:, K_M0:K_M0 + 256] if n == 0 else cst[:, K_M1:K_M1 + 256]
                OP("dve", "scalar_tensor_tensor", [PB[sb_], B_cst], [Bsm], out=sm, in0=bank(sb_)[:, 0:256], scalar=0.125,
                   in1=mk, op0=ALU.mult, op1=ALU.add)
                OP("dve", "reduce_max", [Bsm], [Bst], out=st[:, 0:1], in_=sm, axis=AX.X)
                OP("dve", "tensor_scalar", [Bst, B_cst], [Bst], out=st[:, 1:2], in0=st[:, 0:1],
                   scalar1=sinks[:, head:head + 1], scalar2=-1.0, op0=ALU.max, op1=ALU.mult)
                ACT(p_, sm, AF.Exp, [Bsm, Bst], [Bp, Bst], bias=st[:, 1:2], scale=1.0, accum_out=st[:, 2:3])
                ACT(st[:, 3:4], sinks[:, head:head + 1], AF.Exp, [Bst, B_cst], [Bst], bias=st[:, 1:2], scale=1.0)
                OP("dve", "tensor_tensor", [Bst], [Bst], out=st[:, 4:5], in0=st[:, 2:3], in1=st[:, 3:4], op=ALU.add)
                OP("dve", "reciprocal", [Bst], [Bst], out=st[:, 5:6], in_=st[:, 4:5])
                ACT(pn, p_, AF.Copy, [Bp, Bst], [Bpn], scale=st[:, 5:6])

            def stageB(u, k):
                pn, Bpn = pn_t[k % NS]
                pnT, BpnT = pnT_t[k % NS]
                tbk = 3 + (k % 2)
                tb = bank_bf(tbk)
                TR(tb[:, 0:128], pn[:, 0:128], ident, [Bpn, B_id], [PB[tbk]])
                TR(tb[:, 128:256], pn[:, 128:256], ident, [Bpn, B_id], [PB[tbk]])
                OP("act", "copy", [PB[tbk]], [BpnT], out=pnT, in_=tb[:, 0:256])

            def stageC(u, k):
                hh, n = u
                base = hh * 64
                pnT, BpnT = pnT_t[k % NS]
                oap = psum[base:base + 64, po * 512 + n * 128:po * 512 + (n + 1) * 128]
                pbs = [PB[po + (n // 4)]]
                MM(oap, vat[:, n, :], pnT[:, 0:128], True, False, [B_vat, BpnT], pbs)
                MM(oap, vat[:, n + 1, :], pnT[:, 128:256], False, True, [B_vat, BpnT], pbs)

            k0 = ucount["n"]
            nu = len(units)
            for i in range(nu + 2):
                if i < nu:
                    stageA(units[i], k0 + i)
                if 0 <= i - 1 < nu:
                    stageB(units[i - 1], k0 + i - 1)
                if 0 <= i - 2 < nu:
                    stageC(units[i - 2], k0 + i - 2)
            ucount["n"] = k0 + nu
            OP("act", "copy", [PB[po], PB[po + 1]], [B_oa], out=o_attT[:, g * 2 + jt, :], in_=bank(po, 2))
    if debug:
        finals.append(P.dma("sp", dbg["oatt"], o_attT.rearrange("p a b -> p (a b)"), reads=[B_oa], writes=[Buf("d")]))
    P.barrier()
    if stop_after == "ATT":
        st = P.emit(finals)
        es.close()
        return nc, st

    R_ACT.reset()
    R_X.top = x_mark

    tl_a = hgrn_tiles(R_ACT)
    qs_a = R_ACT.alloc([512], F32, "qs")
    sg_a = R_ACT.alloc([512], BF16, "sg")
    Z, BZ = R_ACT.alloc([4, 512], F32, "Z")
    e0, Be0 = R_ACT.alloc([512], F32, "e0")
    ref16, Bref = R_ACT.alloc([16], F32, "ref16")
    ops_t = {nm: R_X.alloc([512], BF16, nm) for nm in ["qt", "q128", "k128", "q64", "k64", "q32", "k32"]}
    scT, BscT = R_X.alloc([4, 128], BF16, "scT")
    Sbf, BSbf = R_X.alloc([4, 128], BF16, "Sbf")
    osq, Bosq = R_X.alloc([512], BF16, "osq")
    rstd, Brstd = R_X.alloc([512], F32, "rstd")
    t1, Bt1 = R_X.alloc([512], F32, "t1")
    OP("dve", "memset", [], [ops_t["q128"][1]], ops_t["q128"][0], 0.0)
    OP("dve", "memset", [], [ops_t["k128"][1]], ops_t["k128"][0], 0.0)

    def v3(ap, nblk):
        return ap.rearrange("p (a b) -> p a b", a=nblk)


    def alloc_any(shape, dt, name):
        n = 1
        for s_ in shape:
            n *= s_
        words = ((n if dt == F32 else (n + 1) // 2) + 7) // 8 * 8
        for R in (R_ACT, R_X, R_BIG):
            if R.top + words <= R.hi:
                return R.alloc(shape, dt, name)
        raise AssertionError("no room for " + name)

    tl_b = {}
    for nm, shp, dt in [("f", [512], F32), ("lf", [512], F32), ("b", [512], F32), ("ed", [512], BF16),
                        ("ebl", [8], F32), ("kdec", [512], BF16), ("kdT", [4, 128], BF16),
                        ("vfm", [512], BF16), ("vtok", [4, 128], BF16)]:
        tl_b[nm] = alloc_any(shp, dt, nm + "_b")
    qs_b = alloc_any([512], F32, "qs_b")
    sg_b = alloc_any([512], BF16, "sg_b")
    tl2, qs2, sg2 = [tl_a, tl_b], [qs_a, qs_b], [sg_a, sg_b]
    units2 = [(h, half) for h in range(8) for half in range(2)]
    wv_h = {}

    def p2_A(u):
        h, half = units2[u]
        if half == 0:
            wv_h[h] = wget("P2_%d" % h, 16)
        wv, wb = wv_h[h]
        x0 = 128 + half * 512
        hgrn_proj(tl2[u % 2], h, half, wv, wb, 128, 384, 2, 3)
        proj_fm(bank(0), [PB[0]], wv, wb, 0, 128, x0, 512)
        proj_fm(bank(1), [PB[1]], wv, wb, 256, 128, x0, 512)
        if half == 1:
            wload_next()

    def p2_CH(u):
        h, half = units2[u]
        hgrn_chain(tl2[u % 2], h, half, 2, 3)
        ACT(qs2[u % 2][0], bank(0), AF.Silu, [PB[0]], [qs2[u % 2][1]])
        ACT(sg2[u % 2][0], bank(1), AF.Silu, [PB[1]], [sg2[u % 2][1]])

    def p2_S2(u):
        if True:
            h, half = units2[u]
            tl = tl2[u % 2]
            qs, Bqs = qs2[u % 2]
            sg, Bsg = sg2[u % 2]
            pbq, pbsc, pbo = 4, 6, 7
            hgrn_s2(tl, 4, 5)
            pb_s = 5
            f, Bf = tl["f"]
            lf, Blf = tl["lf"]
            b, Bb = tl["b"]
            ebl, Bebl = tl["ebl"]
            vtok, Bvt = tl["vtok"]
            b4, z1 = v3(b, 4), v3(Z[:, 0, :], 4)
            OP("dve", "tensor_tensor", [Bb], [BZ], out=z1[:, :, 64:128], in0=b4[:, :, 64:128],
               in1=b4[:, :, 63:64].broadcast_to([128, 4, 64]), op=ALU.subtract)
            OP("dve", "tensor_tensor", [Bb], [BZ], out=z1[:, :, 0:64], in0=b4[:, :, 63:64].broadcast_to([128, 4, 64]),
               in1=b4[:, :, 0:64], op=ALU.subtract)
            b8, z2 = v3(b, 8), v3(Z[:, 1, :], 8)
            OP("dve", "tensor_tensor", [Bb], [BZ], out=z2[:, :, 32:64], in0=b8[:, :, 32:64],
               in1=b8[:, :, 31:32].broadcast_to([128, 8, 32]), op=ALU.subtract)
            OP("dve", "tensor_tensor", [Bb], [BZ], out=z2[:, :, 0:32], in0=b8[:, :, 31:32].broadcast_to([128, 8, 32]),
               in1=b8[:, :, 0:32], op=ALU.subtract)
            b16, lf16, z3, z4 = v3(b, 16), v3(lf, 16), v3(Z[:, 2, :], 16), v3(Z[:, 3, :], 16)
            OP("dve", "tensor_tensor", [Bb, Blf], [Bref], out=ref16, in0=b16[:, :, 0], in1=lf16[:, :, 0], op=ALU.subtract)
            rbc = ref16.unsqueeze(2).broadcast_to([128, 16, 32])
            OP("dve", "tensor_tensor", [Bb, Bref], [BZ], out=z3, in0=b16, in1=rbc, op=ALU.subtract)
            OP("dve", "tensor_tensor", [Bb, Bref], [BZ], out=z4, in0=rbc, in1=b16, op=ALU.subtract)
            Zf = Z.rearrange("p a b -> p (a b)")
            ACT(Zf, Zf, AF.Exp, [BZ], [BZ])
            ACT(e0, b, AF.Exp, [Bb], [Be0])
            def prod(nm, a, Ba, bb, Bbb, lo=0, hi=128, nblk=4):
                ap, B_ = ops_t[nm]
                if (lo, hi) == (0, 128):
                    OP("dve", "tensor_tensor", [Ba, Bbb], [B_], out=ap, in0=a, in1=bb, op=ALU.mult)
                else:
                    OP("dve", "tensor_tensor", [Ba, Bbb], [B_], out=v3(ap, nblk)[:, :, lo:hi], in0=v3(a, nblk)[:, :, lo:hi],
                       in1=v3(bb, nblk)[:, :, lo:hi], op=ALU.mult)
            prod("qt", qs, Bqs, e0, Be0)
            prod("q128", qs, Bqs, Z[:, 0, :], BZ, 64, 128)
            prod("k128", f, Bf, Z[:, 0, :], BZ, 0, 64)
            prod("q64", qs, Bqs, Z[:, 1, :], BZ)
            prod("k64", f, Bf, Z[:, 1, :], BZ)
            prod("q32", qs, Bqs, Z[:, 2, :], BZ)
            prod("k32", f, Bf, Z[:, 3, :], BZ)
            qt, q128, k128 = ops_t["qt"], ops_t["q128"], ops_t["k128"]
            q64, k64, q32, k32 = ops_t["q64"], ops_t["k64"], ops_t["q32"], ops_t["k32"]
            sc = bank(pbsc)
            MM(sc, zeros_bf, q128[0], True, False, [B_zeros, q128[1]], [PB[pbsc]])
            for c in range(4):
                MM(sc[:, c * 128:(c + 1) * 128], k128[0][:, c * 128:(c + 1) * 128], q128[0][:, c * 128:(c + 1) * 128],
                   False, False, [k128[1], q128[1]], [PB[pbsc]])
            for c in range(4):
                for blk in range(2):
                    o0 = c * 128 + blk * 64
                    MM(psum[blk * 64:blk * 64 + 32, pbsc * 512 + o0 + 32:pbsc * 512 + o0 + 64], k64[0][:, o0:o0 + 32],
                       q64[0][:, o0 + 32:o0 + 64], False, False, [k64[1], q64[1]], [PB[pbsc]], tile_position=(0, blk * 64))
                for blk in range(4):
                    o0 = c * 128 + blk * 32
                    MM(psum[blk * 32:blk * 32 + 32, pbsc * 512 + o0:pbsc * 512 + o0 + 32], k32[0][:, o0:o0 + 32],
                       q32[0][:, o0:o0 + 32], False, False, [k32[1], q32[1]], [PB[pbsc]],
                       tile_position=(0, blk * 32))
            MM(sc, zeros_bf, q128[0], False, True, [B_zeros, q128[1]], [PB[pbsc]])
            OP("dve", "tensor_tensor", [PB[pbsc], B_cst], [BscT], out=scT, in0=v3(sc, 4),
               in1=cmask.unsqueeze(1).broadcast_to([128, 4, 128]), op=ALU.mult)
            for c in range(4):
                OP("dve", "tensor_copy", [B_T], [BSbf], out=Sbf[:, c, :], in_=T[:, h, :])
                OP("dve", "scalar_tensor_tensor", [B_T, Bebl, PB[pb_s]], [B_T], out=T[:, h, :], in0=T[:, h, :],
                   scalar=ebl[:, half * 4 + c:half * 4 + c + 1], in1=bank(pb_s)[:, c * 128:(c + 1) * 128],
                   op0=ALU.mult, op1=ALU.add)
            ob = bank(pbo)
            for c in range(4):
                MM(ob[:, c * 128:(c + 1) * 128], Sbf[:, c, :], qt[0][:, c * 128:(c + 1) * 128], True, False,
                   [BSbf, qt[1]], [PB[pbo]])
                MM(ob[:, c * 128:(c + 1) * 128], vtok[:, c, :], scT[:, c, :], False, True, [Bvt, BscT], [PB[pbo]])
            ACT(osq, ob, AF.Square, [PB[pbo]], [Bosq])
            MM(bank(pbq), ones_bf, osq, True, True, [B_ones, Bosq], [PB[pbq]])
            ACT(rstd, bank(pbq), AF.Ln, [PB[pbq], B_vec], [Brstd], bias=eps_rms, scale=1.0 / 128.0)
            ACT(rstd, rstd, AF.Exp, [Brstd], [Brstd], scale=-0.5)
            OP("dve", "scalar_tensor_tensor", [PB[pbo], B_cst, Brstd], [Bt1], out=t1, in0=ob, scalar=gain[:, h:h + 1],
               in1=rstd, op0=ALU.mult, op1=ALU.mult)
            OP("dve", "tensor_tensor", [Bt1, Bsg], [B_oh], out=o_hT[:, h, half * 512:(half + 1) * 512], in0=t1, in1=sg,
               op=ALU.mult)

    p2_A(0)
    p2_CH(0)
    for u in range(16):
        if u + 1 < 16:
            p2_A(u + 1)
        p2_S2(u)
        if u + 1 < 16:
            p2_CH(u + 1)
    if debug:
        finals.append(P.dma("sp", dbg["oh"], o_hT.rearrange("p a b -> p (a b)"), reads=[B_oh], writes=[Buf("d")]))
    P.barrier()
    if stop_after == "A":
        st = P.emit(finals)
        es.close()
        return nc, st

    R_ACT.reset()
    R_X.top = x_mark
    mixedT, B_mx = R_ACT.alloc([16, TPC], BF16, "mixedT")
    sga, Bsga = R_X.alloc([TPC], F32, "sga")
    sgb, Bsgb = R_X.alloc([TPC], F32, "sgb")
    mtmp, Bmt = R_X.alloc([TPC], F32, "mtmp")
    for i in range(8):
        wvg, wbg = wget("G_%d" % i, 16)
        wvab, wbab = wget("AB_%d" % i, 16)
        for j in range(2):
            dc = 2 * i + j
            for half in range(2):
                proj_fm(bank(half), [PB[half]], wvg, wbg, j * 128, 128, 128 + half * 512, 512)
            ACT(sga, bank(0, 2), AF.Sigmoid, [PB[0], PB[1]], [Bsga])
            for half in range(2):
                proj_fm(bank(2 + half), [PB[2 + half]], wvg, wbg, 256 + j * 128, 128, 128 + half * 512, 512)
            ACT(sgb, bank(2, 2), AF.Sigmoid, [PB[2], PB[3]], [Bsgb])
            for half in range(2):
                for kc in range(8):
                    MM(bank(4 + half), wvab[:, kc, j * 128:(j + 1) * 128], o_hT[:, kc, half * 512:(half + 1) * 512],
                       kc == 0, kc == 7, [wbab, B_oh], [PB[4 + half]])
            OP("dve", "tensor_tensor", [PB[4], PB[5], Bsga], [Bmt], out=mtmp, in0=bank(4, 2), in1=sga, op=ALU.mult)
            for half in range(2):
                for kc in range(8):
                    MM(bank(6 + half), wvab[:, 8 + kc, j * 128:(j + 1) * 128], o_attT[:, kc, half * 512:(half + 1) * 512],
                       kc == 0, kc == 7, [wbab, B_oa], [PB[6 + half]])
            OP("dve", "tensor_tensor", [PB[6], PB[7], Bsgb], [Bsgb], out=sgb, in0=bank(6, 2), in1=sgb, op=ALU.mult)
            OP("dve", "tensor_tensor", [Bsgb, Bmt], [B_mx], out=mixedT[:, dc, :], in0=sgb, in1=mtmp, op=ALU.add)
        wload_next()
        wload_next()
    if debug:
        finals.append(P.dma("sp", dbg["mixed"], mixedT.rearrange("p a b -> p (a b)"), reads=[B_mx], writes=[Buf("d")]))
    P.barrier()
    if stop_after == "B":
        st = P.emit(finals)
        es.close()
        return nc, st

    R_BIG.reset()
    R_X.reset()
    xres, B_xr = R_BIG.alloc([8, D], F32, "xres")
    B_xrt = [Buf("xres%d" % t_) for t_ in range(8)]
    lng, B_lng = R_X.alloc([D], F32, "lng")
    lnb, B_lnb = R_X.alloc([D], F32, "lnb")
    for t_ in range(8):
        P.dma("sp", xres[:, t_, :], xtok_d[:, t_ * D:(t_ + 1) * D], writes=[B_xrt[t_]])
    P.dma("sp", lng, lnp_d[0:1, :].broadcast_to([128, D]), writes=[B_lng])
    P.dma("sp", lnb, lnp_d[1:2, :].broadcast_to([128, D]), writes=[B_lnb])
    bk_rr = {"n": 0}
    for cb in range(4):
        wv, wb = wget("WO_%d" % cb, 16)
        for t_ in range(8):
            bk = bk_rr["n"] % 4
            bk_rr["n"] += 1
            for kc in range(16):
                MM(bank(bk), mixedT[:, kc, t_ * 128:(t_ + 1) * 128], wv[:, kc, :], kc == 0, kc == 15, [B_mx, wb], [PB[bk]])
            OP("dve", "scalar_tensor_tensor", [B_xrt[t_], PB[bk]], [B_xrt[t_]], out=xres[:, t_, cb * 512:(cb + 1) * 512],
               in0=xres[:, t_, cb * 512:(cb + 1) * 512], scalar=ALPHA, in1=bank(bk), op0=ALU.mult, op1=ALU.add)
        wload_next()
    P.barrier()
    R_ACT.reset()
    x1T, B_x1T = R_ACT.alloc([16, TPC], BF16, "x1T")
    stats, Bstat = R_X.alloc([2, 32], F32, "stats")
    xbf = [R_X.alloc([D], BF16, "xbf%d" % i) for i in range(2)]

    def layer_norm(t_, gtile, Bg_, btile, Bb_, k):
        x = xres[:, t_, :]
        Bx = B_xrt[t_]
        st = stats[:, k % 2, :]
        for c4 in range(4):
            OP("dve", "bn_stats", [Bx], [Bstat], out=st[:, c4 * 6:(c4 + 1) * 6], in_=x[:, c4 * 512:(c4 + 1) * 512])
        OP("dve", "bn_aggr", [Bstat], [Bstat], out=st[:, 24:26], in_=st[:, 0:24])
        ACT(st[:, 26:27], st[:, 25:26], AF.Sqrt, [Bstat, B_vec], [Bstat], bias=eps_ln, scale=1.0)
        OP("dve", "reciprocal", [Bstat], [Bstat], out=st[:, 27:28], in_=st[:, 26:27])
        OP("dve", "tensor_scalar", [Bx, Bstat], [Bx], out=x, in0=x, scalar1=st[:, 24:25], scalar2=st[:, 27:28],
           op0=ALU.subtract, op1=ALU.mult)
        OP("dve", "tensor_tensor", [Bx, Bg_], [Bx], out=x, in0=x, in1=gtile, op=ALU.mult)
        OP("dve", "tensor_tensor", [Bx, Bb_], [Bx], out=x, in0=x, in1=btile, op=ALU.add)

    for t_ in range(8):
        layer_norm(t_, lng, B_lng, lnb, B_lnb, t_)
        xb, Bxb = xbf[t_ % 2]
        OP("act", "copy", [B_xrt[t_]], [Bxb], out=xb, in_=xres[:, t_, :])
        for hb in range(2):
            tbk = 4 + ((2 * t_ + hb) % 2)
            tb = bank_bf(tbk)
            for kk_ in range(8):
                kc = hb * 8 + kk_
                TR(tb[:, kk_ * 128:(kk_ + 1) * 128], xb[:, kc * 128:(kc + 1) * 128], ident, [Bxb, B_id], [PB[tbk]])
            OP("act" if hb == 0 else "dve", "copy" if hb == 0 else "tensor_copy", [PB[tbk]], [B_x1T],
               out=x1T[:, hb * 8:(hb + 1) * 8, t_ * 128:(t_ + 1) * 128], in_=tb.rearrange("p (a b) -> p a b", a=8))
        if debug:
            finals.append(P.dma("sp", dbg["x1"][:, t_ * D:(t_ + 1) * D], xres[:, t_, :], reads=[B_xrt[t_]], writes=[Buf("d")]))
        OP("act", "mul", [B_xrt[t_]], [B_xrt[t_]], out=xres[:, t_, :], in_=xres[:, t_, :], mul=ALPHA)
    P.barrier()
    if stop_after == "C":
        st = P.emit(finals)
        es.close()
        return nc, st

    R_X.reset()
    lng2, B_lng2 = R_X.alloc([D], F32, "lng2")
    lnb2, B_lnb2 = R_X.alloc([D], F32, "lnb2")
    P.dma("sp", lng2, lnp_d[2:3, :].broadcast_to([128, D]), writes=[B_lng2])
    P.dma("sp", lnb2, lnp_d[3:4, :].broadcast_to([128, D]), writes=[B_lnb2])
    hT = [R_X.alloc([4, TPC], BF16, "hT%d" % i) for i in range(2)]
    rl = [R_X.alloc([512], F32, "rl%d" % i) for i in range(2)]
    stats, Bstat = R_X.alloc([2, 32], F32, "stats2")
    hb_rr = {"n": 0}
    yb_rr = {"n": 0}
    for fb in range(16):
        w1v, w1b = wget("F1_%d" % fb, 16)
        w2v, w2b = wget("F2_%d" % fb, 4)
        h_, Bh = hT[fb % 2]
        for fc in range(4):
            for half in range(2):
                bk = hb_rr["n"] % 4
                hb_rr["n"] += 1
                r_, Br = rl[hb_rr["n"] % 2]
                for kc in range(16):
                    MM(bank(bk), w1v[:, kc, fc * 128:(fc + 1) * 128], x1T[:, kc, half * 512:(half + 1) * 512], kc == 0,
                       kc == 15, [w1b, B_x1T], [PB[bk]])
                ACT(r_, bank(bk), AF.Relu, [PB[bk]], [Br])
                ACT(h_[:, fc, half * 512:(half + 1) * 512], r_, AF.Square, [Br], [Bh])
        for t_ in range(8):
            for cb in range(4):
                bk = 4 + yb_rr["n"] % 4
                yb_rr["n"] += 1
                for fc in range(4):
                    MM(bank(bk), h_[:, fc, t_ * 128:(t_ + 1) * 128], w2v[:, fc, cb * 512:(cb + 1) * 512], fc == 0, fc == 3,
                       [Bh, w2b], [PB[bk]])
                OP("dve", "tensor_tensor", [B_xrt[t_], PB[bk]], [B_xrt[t_]], out=xres[:, t_, cb * 512:(cb + 1) * 512],
                   in0=xres[:, t_, cb * 512:(cb + 1) * 512], in1=bank(bk), op=ALU.add)
            if fb == 15:
                layer_norm(t_, lng2, B_lng2, lnb2, B_lnb2, t_)
                finals.append(P.dma("sp", out_d[t_ * 128:(t_ + 1) * 128, :], xres[:, t_, :], reads=[B_xrt[t_]],
                                    writes=[Buf("o")]))
        wload_next()
        wload_next()
    st = P.emit(finals)
    es.close()
    return nc, st


def make_in_maps(x, w_in, hg_lb_logits, hg_norm_gain, attn_sinks, w_branch_a, w_branch_b, w_out, ln1_gain, ln1_bias,
                 w_ff1, w_ff2, ln2_gain, ln2_bias):
    f32 = np.float32
    x2 = np.asarray(x, f32).reshape(S_TOT, D)
    wflat = _build_wflat(np.asarray(w_in, f32)[0], np.asarray(w_branch_a, f32)[0], np.asarray(w_branch_b, f32)[0],
                         np.asarray(w_out, f32)[0], np.asarray(w_ff1, f32)[0], np.asarray(w_ff2, f32)[0])
    wfa = np.ascontiguousarray(wflat[:, :W_SPLIT])
    wfb = np.ascontiguousarray(wflat[:, W_SPLIT:])
    del wflat
    lnp = np.stack([np.asarray(ln1_gain, f32)[0], np.asarray(ln1_bias, f32)[0], np.asarray(ln2_gain, f32)[0],
                    np.asarray(ln2_bias, f32)[0]]).astype(f32)
    lbl = np.asarray(hg_lb_logits, f32)
    gn = np.asarray(hg_norm_gain, f32)[0]
    sk = np.asarray(attn_sinks, f32)[0]
    maps = []
    for c in range(NCORES):
        t0 = c * TPC
        xe = np.zeros((XW, D), f32)
        if c > 0:
            xe[:128] = x2[t0 - 128:t0]
        xe[128:] = x2[t0:t0 + TPC]
        xT = np.ascontiguousarray(xe.T.reshape(16, 128, XW).transpose(1, 0, 2)).reshape(128, 16 * XW)
        xtok = np.ascontiguousarray(x2[t0:t0 + TPC].reshape(8, 128, D).transpose(1, 0, 2)).reshape(128, 8 * D)
        xp = np.zeros((7 * TPC, D), f32)
        if c > 0:
            xp[(7 - c) * TPC:] = x2[:c * TPC]
        xpre = np.ascontiguousarray(xp.reshape(14, 512, 16, 128).transpose(3, 0, 2, 1)).reshape(128, 14 * 16 * 512)
        maps.append({"xT": xT, "xtok": xtok, "xpre": xpre, "wflat": wfa, "wflat2": wfb, "cst": _build_consts(c, lbl, gn, sk), "lnp": lnp})
    return maps


_CACHE = {}


def kernel(**inputs):
    if "nc" not in _CACHE:
        _CACHE["nc"] = build_program()[0]
    nc = _CACHE["nc"]
    maps = make_in_maps(**inputs)
    res = run_bass_kernel_spmd(nc, maps, core_ids=list(range(NCORES)))
    out = np.concatenate([np.asarray(r["out"], np.float32) for r in res.results], axis=0)
    return out.reshape(1, S_TOT, D)
```
